# Optimizing a Trainium2 kernel written in Bass

```python
import math
import jax, jax.numpy as jnp
from jax import lax
import numpy as np

D_MODEL = 2048
BATCH = 32
SEQ = 256
DEPTH = 4
DEC_BATCH = 8
DEC_SEQ = 2048
PAST_LEN = 512

GRID_W = 64
N_MIXERS = 2
N_CONV_LAYERS = (DEPTH + 1) // 2
N_ATTN_LAYERS = DEPTH // 2
D_CONV = D_MODEL
CONV_WIDTH = 3
N_HEADS = 16
QK_DIM = 64
V_DIM = 2 * QK_DIM
QK_W = N_HEADS * 2 * QK_DIM
V_W = N_HEADS * V_DIM
ROPE_AXIS_DIM = QK_DIM // 2
ROPE_BASE = 10000.0
Q_BLOCK = 128
LN_EPS = 1e-5
SUBLN_EPS = 1e-5
DEEPNORM_ALPHA = (2.0 * DEPTH) ** 0.25
DEEPNORM_BETA = (8.0 * DEPTH) ** -0.25

kernel_name = "hybrid_diffusion_conv_diffattn_step"


def layer_norm(x, g, b):
    x32 = x.astype(jnp.float32)
    mu = jnp.mean(x32, -1, keepdims=True)
    var = jnp.mean(jnp.square(x32 - mu), -1, keepdims=True)
    y = (x32 - mu) * lax.rsqrt(var + LN_EPS) * g.astype(jnp.float32) + b.astype(jnp.float32)
    return y.astype(x.dtype)


def conv3_centred(u, w):
    up = jnp.pad(u, ((0, 0), (1, 1), (0, 0)))
    return up[:, :-2] * w[0] + up[:, 1:-1] * w[1] + up[:, 2:] * w[2]


def conv_mixer(h, w_in, w_conv, w_out):
    proj = h @ w_in
    b_gate, c_gate, u, z = jnp.split(proj, 4, axis=-1)
    y = b_gate * conv3_centred(c_gate * u, w_conv) * jax.nn.silu(z)
    return y @ w_out


def axial_rope(x, n_tokens):
    rows = n_tokens // GRID_W
    r, col = jnp.meshgrid(jnp.arange(rows), jnp.arange(GRID_W), indexing='ij')
    r = r.reshape(-1).astype(jnp.float32)
    col = col.reshape(-1).astype(jnp.float32)
    half = ROPE_AXIS_DIM // 2
    inv = ROPE_BASE ** (-jnp.arange(half, dtype=jnp.float32) / half)

    def rot(xa, pos):
        ang = pos[:, None] * inv
        cos = jnp.cos(ang)[None, :, None, None, :]
        sin = jnp.sin(ang)[None, :, None, None, :]
        x1, x2 = xa[..., :half], xa[..., half:]
        return jnp.concatenate([x1 * cos - x2 * sin, x2 * cos + x1 * sin], -1)

    x32 = x.astype(jnp.float32)
    out = jnp.concatenate([rot(x32[..., :ROPE_AXIS_DIM], r), rot(x32[..., ROPE_AXIS_DIM:], col)], -1)
    return out.astype(x.dtype)


def diff_lambda(lam_params, lambda_init):
    lp = lam_params.astype(jnp.float32)
    return jnp.exp(jnp.sum(lp[0] * lp[1])) - jnp.exp(jnp.sum(lp[2] * lp[3])) + lambda_init


def diff_attend(q, k, v, lam):
    B, Lq = q.shape[0], q.shape[1]
    nb = Lq // Q_BLOCK
    qb = q.reshape(B, nb, Q_BLOCK, N_HEADS, 2, QK_DIM).swapaxes(0, 1)
    scale = QK_DIM ** -0.5
    lam32 = lam.astype(jnp.float32)

    def block(qi):
        s = jnp.einsum('bqhcd,bkhcd->bhcqk', qi, k).astype(jnp.float32) * scale
        p = jax.nn.softmax(s, axis=-1)
        pd = p[:, :, 0] - lam32 * p[:, :, 1]
        return jnp.einsum('bhqk,bkhe->bqhe', pd.astype(v.dtype), v)

    o = lax.map(block, qb)
    return o.swapaxes(0, 1).reshape(B, Lq, N_HEADS, V_DIM)


def attn_project(h, w_in):
    B, L = h.shape[0], h.shape[1]
    proj = h @ w_in
    q, k, v, z = jnp.split(proj, [QK_W, 2 * QK_W, 2 * QK_W + V_W], axis=-1)
    return (q.reshape(B, L, N_HEADS, 2, QK_DIM), k.reshape(B, L, N_HEADS, 2, QK_DIM),
            v.reshape(B, L, N_HEADS, V_DIM), z)


def attn_finish(o, z, subln_w, lambda_init, w_out):
    o32 = o.astype(jnp.float32)
    o32 = o32 * lax.rsqrt(jnp.mean(jnp.square(o32), -1, keepdims=True) + SUBLN_EPS)
    o32 = o32 * subln_w.astype(jnp.float32) * (1.0 - lambda_init)
    B, L = o.shape[0], o.shape[1]
    y = o32.astype(z.dtype).reshape(B, L, V_W) * jax.nn.silu(z)
    return y @ w_out


def setup_inputs(seed: int = 0) -> dict:
    key = jax.random.key(seed)
    ks = jax.random.split(key, 20)
    f32 = jnp.float32
    nrm = lambda k, s: jax.random.normal(k, s, f32)
    d = D_MODEL
    return {
        "x_prompt": nrm(ks[0], (BATCH, SEQ, d)),
        "x_sample": nrm(ks[1], (DEC_BATCH, DEC_SEQ, d)),
        "cache_k": nrm(ks[2], (DEC_BATCH, N_ATTN_LAYERS, PAST_LEN, N_HEADS, 2, QK_DIM)),
        "cache_v": nrm(ks[3], (DEC_BATCH, N_ATTN_LAYERS, PAST_LEN, N_HEADS, V_DIM)),
        "c": nrm(ks[4], (DEC_BATCH, d)),
        "c_ctx": nrm(ks[5], (d,)),
        "ada_w": nrm(ks[6], (DEPTH, d, 3 * d)) * (0.5 * d ** -0.5),
        "ada_b": nrm(ks[7], (DEPTH, 3 * d)) * 0.02,
        "ln_g": 1.0 + 0.02 * nrm(ks[8], (DEPTH, d)),
        "ln_b": 0.02 * nrm(ks[9], (DEPTH, d)),
        "conv_w_in": nrm(ks[10], (N_CONV_LAYERS, d, 4 * D_CONV)) * d ** -0.5,
        "conv_w": nrm(ks[11], (N_CONV_LAYERS, CONV_WIDTH, D_CONV)) * CONV_WIDTH ** -0.5,
        "conv_w_out": nrm(ks[12], (N_CONV_LAYERS, D_CONV, d)) * (DEEPNORM_BETA * D_CONV ** -0.5),
        "attn_w_in": nrm(ks[13], (N_ATTN_LAYERS, d, 2 * QK_W + 2 * V_W)) * d ** -0.5,
        "attn_lambda": 0.1 * nrm(ks[14], (N_ATTN_LAYERS, 4, QK_DIM)),
        "attn_subln_w": 1.0 + 0.02 * nrm(ks[15], (N_ATTN_LAYERS, V_DIM)),
        "attn_w_out": nrm(ks[16], (N_ATTN_LAYERS, V_W, d)) * (DEEPNORM_BETA * V_W ** -0.5),
    }


def reference(x_prompt, x_sample, cache_k, cache_v, c, c_ctx, ada_w, ada_b, ln_g, ln_b,
              conv_w_in, conv_w, conv_w_out, attn_w_in, attn_lambda, attn_subln_w, attn_w_out):
    xp, xs = x_prompt, x_sample
    n_lat = xs.shape[1]
    s_ctx = jax.nn.silu(c_ctx)
    s_lat = jax.nn.silu(c)
    new_k, new_v = [], []
    for i in range(DEPTH):
        j = i // N_MIXERS
        mod_p = s_ctx @ ada_w[i] + ada_b[i]
        mod_s = (s_lat @ ada_w[i] + ada_b[i])[:, None, :]
        sh_p, sc_p, g_p = jnp.split(mod_p, 3, axis=-1)
        sh_s, sc_s, g_s = jnp.split(mod_s, 3, axis=-1)
        hp = xp * (1.0 + sc_p) + sh_p
        hs = xs * (1.0 + sc_s) + sh_s
        if i % N_MIXERS == 0:
            op = conv_mixer(hp, conv_w_in[j], conv_w[j], conv_w_out[j])
            os_ = conv_mixer(hs, conv_w_in[j], conv_w[j], conv_w_out[j])
        else:
            lam_init = 0.8 - 0.6 * math.exp(-0.3 * i)
            lam = diff_lambda(attn_lambda[j], lam_init)
            qp, kp, vp, zp = attn_project(hp, attn_w_in[j])
            op = attn_finish(diff_attend(qp, kp, vp, lam), zp, attn_subln_w[j], lam_init, attn_w_out[j])
            new_k.append(kp)
            new_v.append(vp)
            qs, ks_, vs, zs = attn_project(hs, attn_w_in[j])
            qs = axial_rope(qs, n_lat)
            ks_ = axial_rope(ks_, n_lat)
            k_all = jnp.concatenate([ks_, cache_k[:, j].astype(ks_.dtype)], axis=1)
            v_all = jnp.concatenate([vs, cache_v[:, j].astype(vs.dtype)], axis=1)
            os_ = attn_finish(diff_attend(qs, k_all, v_all, lam), zs, attn_subln_w[j], lam_init, attn_w_out[j])
        xp = layer_norm(DEEPNORM_ALPHA * xp + g_p * op, ln_g[i], ln_b[i])
        xs = layer_norm(DEEPNORM_ALPHA * xs + g_s * os_, ln_g[i], ln_b[i])
    new_cache_k = jnp.stack(new_k, axis=1)
    new_cache_v = jnp.stack(new_v, axis=1)
    return (xp, xs, new_cache_k, new_cache_v)
```

```python
import math
import os
import numpy as np
import concourse.bass as bass
import concourse.mybir as mybir
from concourse.bass_utils import run_bass_kernel_spmd

F32 = mybir.dt.float32
BF16 = mybir.dt.bfloat16
AF = mybir.ActivationFunctionType
ALU = mybir.AluOpType
AX = mybir.AxisListType

D = 2048
DEPTH = 4
NCORES = 8
LS = 2048
LP = 256
SP_PER_CORE = 4
PAST = 512
ALPHA = (2.0 * DEPTH) ** 0.25
LN_EPS = 1e-5
SELF_SYNC = True
N_LAYERS_RUN = DEPTH


class Eng:
    def __init__(self, nc, e, name):
        self.e = e
        self.name = name
        self.sem = nc.alloc_semaphore("tk_" + name)
        self.t = 0
        self.seen = {}

    def wait(self, dep):
        sem, val, key = dep
        if key == self.name and (self.name == "pe" or not SELF_SYNC):
            return
        if self.seen.get(key, 0) >= val:
            return
        self.seen[key] = val
        self.e.wait_ge(sem, val)


class DSem:
    def __init__(self, nc, name):
        self.handle = nc.alloc_semaphore(name)
        self.count = 0
        self.name = name


class Res:
    def __init__(self, name, ap=None, ds=None):
        self.name = name
        self.ap = ap
        self.ds = ds
        self.wr = {}
        self.rd = {}


class Group:
    def __init__(self, idx, S, L, cache):
        self.idx = idx
        self.S = S
        self.L = L
        self.T = S * L
        self.cache = cache


class KB:
    def __init__(self, nc):
        self.nc = nc
        self.PE = Eng(nc, nc.tensor, "pe")
        self.ACT = Eng(nc, nc.scalar, "act")
        self.DVE = Eng(nc, nc.vector, "dve")
        self.POOL = Eng(nc, nc.gpsimd, "pool")
        self.SP = Eng(nc, nc.sync, "sp")
        self.engs = [self.PE, self.ACT, self.DVE, self.POOL, self.SP]
        self.free_ds = []
        self.borrowed = []
        self.nds = 0

    def _waits(self, eng, reads, writes):
        for r in reads:
            for d in list(r.wr.values()):
                eng.wait(d)
        for w in writes:
            for d in list(w.wr.values()):
                eng.wait(d)
            for d in list(w.rd.values()):
                eng.wait(d)

    def begin(self, eng, reads=(), writes=()):
        self._waits(eng, reads, writes)

    def end(self, eng, ins, reads=(), writes=()):
        eng.t += 1
        ins.then_inc(eng.sem, 1)
        d = (eng.sem, eng.t, eng.name)
        for r in reads:
            r.rd[eng.name] = d
        for w in writes:
            w.wr[eng.name] = d

    def emit(self, eng, mk, reads=(), writes=()):
        self._waits(eng, reads, writes)
        ins = mk()
        self.end(eng, ins, reads, writes)
        return ins

    def dma(self, q, out, in_, slot, reads=(), writes=()):
        self._waits(q, reads, writes)
        ins = q.e.dma_start(out=out, in_=in_)
        ds = slot.ds
        ds.count += 16
        ins.then_inc(ds.handle, 16)
        d = (ds.handle, ds.count, ds.name)
        for r in reads:
            r.rd[ds.name] = d
        for w in writes:
            w.wr[ds.name] = d

    def borrow_ds(self):
        if self.free_ds:
            ds = self.free_ds.pop()
        else:
            ds = DSem(self.nc, "ds%d" % self.nds)
            self.nds += 1
        self.borrowed.append(ds)
        return ds

    def barrier(self):
        for E in self.engs:
            for Fg in self.engs:
                if Fg is not E and Fg.t > 0:
                    E.wait((Fg.sem, Fg.t, Fg.name))
            for ds in self.borrowed:
                if ds.count > 0:
                    E.wait((ds.handle, ds.count, ds.name))
        self.free_ds.extend(self.borrowed)
        self.borrowed = []


class Arena:
    def __init__(self, kb, tensor, nwords):
        self.kb = kb
        self.t = tensor
        self.n = nwords
        self.off = 0

    def f32(self, n, name, dma=False):
        n2 = (n + 7) // 8 * 8
        assert self.off + n2 <= self.n, ("arena overflow", name, self.off, n2, self.n)
        ap = self.t[:, self.off:self.off + n]
        self.off += n2
        return Res(name, ap, self.kb.borrow_ds() if dma else None)

    def bf16(self, n, name, dma=False):
        nw = (n + 1) // 2
        n2 = (nw + 7) // 8 * 8
        assert self.off + n2 <= self.n, ("arena overflow", name, self.off, n2, self.n)
        ap = self.t[:, self.off:self.off + nw].bitcast(BF16)
        self.off += n2
        return Res(name, ap, self.kb.borrow_ds() if dma else None)


def dram_bcast(handle, offset, nparts, n):
    return bass.AP(handle, offset, [[0, nparts], [1, n]])


def build_program():
    nc = bass.Bass("TRN2", target_bir_lowering=False)
    kb = KB(nc)
    PE, ACT, DVE, POOL, SP = kb.PE, kb.ACT, kb.DVE, kb.POOL, kb.SP
    emit, begin, end, dma = kb.emit, kb.begin, kb.end, kb.dma

    xs_in = nc.dram_tensor("xs", [LS, D], F32, kind="ExternalInput")
    xp_in = nc.dram_tensor("xp", [SP_PER_CORE * LP, D], F32, kind="ExternalInput")
    ck_in = nc.dram_tensor("ck", [2, PAST, D], F32, kind="ExternalInput")
    cv_in = nc.dram_tensor("cv", [2, PAST, D], F32, kind="ExternalInput")
    svT_in = nc.dram_tensor("svT", [128, 32], F32, kind="ExternalInput")
    ada_w = nc.dram_tensor("ada_w", [DEPTH, D, 3 * D], F32, kind="ExternalInput")
    ada_b = nc.dram_tensor("ada_b", [DEPTH, 3 * D], F32, kind="ExternalInput")
    ln_g = nc.dram_tensor("ln_g", [DEPTH, D], F32, kind="ExternalInput")
    ln_b = nc.dram_tensor("ln_b", [DEPTH, D], F32, kind="ExternalInput")
    conv_w_in = nc.dram_tensor("conv_w_in", [2, D, 4 * D], F32, kind="ExternalInput")
    conv_wT = nc.dram_tensor("conv_wT", [128, 2 * 16 * 3], F32, kind="ExternalInput")
    conv_w_out = nc.dram_tensor("conv_w_out", [2, D, D], F32, kind="ExternalInput")
    attn_w_in = nc.dram_tensor("attn_w_in", [2, D, 4 * D], F32, kind="ExternalInput")
    attn_lam = nc.dram_tensor("attn_lam", [2, 256], F32, kind="ExternalInput")
    subwT_in = nc.dram_tensor("subwT", [128, 2], F32, kind="ExternalInput")
    attn_w_out = nc.dram_tensor("attn_w_out", [2, D, D], F32, kind="ExternalInput")
    cos_in = nc.dram_tensor("cosT", [128, LS], F32, kind="ExternalInput")
    sin_in = nc.dram_tensor("sinT", [128, LS], F32, kind="ExternalInput")
    cmat_in = nc.dram_tensor("cmat", [128, 3 * 128], F32, kind="ExternalInput")

    ys_out = nc.dram_tensor("ys", [LS, D], F32, kind="ExternalOutput")
    yp_out = nc.dram_tensor("yp", [SP_PER_CORE * LP, D], F32, kind="ExternalOutput")
    nk_out = nc.dram_tensor("nk", [SP_PER_CORE, 2, LP, D], F32, kind="ExternalOutput")
    nv_out = nc.dram_tensor("nv", [SP_PER_CORE, 2, LP, D], F32, kind="ExternalOutput")

    xa_s = nc.dram_tensor("xa_s", [LS, D], F32)
    xb_s = nc.dram_tensor("xb_s", [LS, D], F32)
    xa_p = nc.dram_tensor("xa_p", [SP_PER_CORE * LP, D], F32)
    xb_p = nc.dram_tensor("xb_p", [SP_PER_CORE * LP, D], F32)
    y_s = nc.dram_tensor("y_s", [16, 128, LS], BF16)
    y_p = nc.dram_tensor("y_p", [16, 128, SP_PER_CORE * LP], BF16)
    modd = nc.dram_tensor("modd", [DEPTH * 2 * 3 * D], F32)

    R_modd = Res("modd")
    gP = Group(0, SP_PER_CORE, LP, False)
    gS = Group(1, 1, LS, True)
    for g_, xin, xa, xb, xout in ((gS, xs_in, xa_s, xb_s, ys_out), (gP, xp_in, xa_p, xb_p, yp_out)):
        ra_, rb_ = Res("xa"), Res("xb")
        g_.X = [xin] + [(xa, xb)[k % 2] for k in range(N_LAYERS_RUN - 1)] + [xout]
        g_.XR = [Res("xin")] + [(ra_, rb_)[k % 2] for k in range(N_LAYERS_RUN - 1)] + [Res("xout")]
    gS.Y, gP.Y = y_s, y_p
    gS.YR, gP.YR = Res("ys_y"), Res("yp_y")
    import os
    _kg = os.environ.get('KGROUPS', 'sp')
    groups = [g_ for g_, c_ in ((gS, 's'), (gP, 'p')) if c_ in _kg]

    bigA_t = nc.alloc_sbuf_tensor("bigA", [128, 16, 2048], BF16)
    R_bigA = Res("bigA", None, DSem(nc, "ds_bigA"))
    wbuf_t = [nc.alloc_sbuf_tensor("wbuf%d" % s, [128, 4, 16, 128], BF16) for s in range(2)]
    R_wbuf = [Res("wbuf%d" % s, None, DSem(nc, "ds_wbuf%d" % s)) for s in range(2)]
    ARENA_WORDS = 24576
    arena_t = nc.alloc_sbuf_tensor("arena", [128, ARENA_WORDS], F32)
    cst_bf = nc.alloc_sbuf_tensor("cst_bf", [128, 3 * 128], BF16)
    ones32 = nc.alloc_sbuf_tensor("ones32", [128, 128], F32)
    smalls = nc.alloc_sbuf_tensor("smalls", [128, 256], F32)
    convw_sb = nc.alloc_sbuf_tensor("convw_sb", [128, 96], F32)
    R_cst = Res("cst", None, DSem(nc, "ds_cst"))
    R_small = Res("smalls", None, DSem(nc, "ds_small"))
    ident_bf = cst_bf[:, 0:128]
    perm_bf = cst_bf[:, 128:256]
    ones_bf = cst_bf[:, 256:384]
    C_EPS = 0
    C_SUBW = 2
    C_NLAM = 4
    C_SV = 32
    C_TMP = 64
    psum_t = nc.alloc_psum_tensor("ps", [128, 4096], F32)
    BANK = [Res("bank%d" % k) for k in range(8)]

    def bank_ap(k, n=512):
        return psum_t[:, k * 512:k * 512 + n]

    dma(POOL, cst_bf[:, :], cmat_in[:, :], R_cst, writes=[R_cst])
    dma(SP, smalls[:, C_SV:C_SV + 32], svT_in[:, :], R_small, writes=[R_small])
    dma(SP, convw_sb[:, :], conv_wT[:, :], R_small, writes=[R_small])
    dma(SP, smalls[:, C_SUBW:C_SUBW + 2], subwT_in[:, :], R_small, writes=[R_small])
    emit(DVE, lambda: nc.vector.memset(ones32[:, :], 1.0), writes=[R_cst])
    emit(DVE, lambda: nc.vector.memset(smalls[:, C_EPS:C_EPS + 1], LN_EPS), writes=[R_small])
    emit(ACT, lambda: nc.scalar.activation(out=smalls[:, C_SV:C_SV + 32], in_=smalls[:, C_SV:C_SV + 32],
                                           func=AF.Silu), reads=[R_small], writes=[R_small])
    ar = Arena(kb, arena_t, ARENA_WORDS)
    lamb = ar.f32(512, "lamb", dma=True)
    dma(SP, lamb.ap, dram_bcast(attn_lam, 0, 128, 512), lamb, writes=[lamb])
    for j in range(2):
        i_layer = 2 * j + 1
        lam_init = 0.8 - 0.6 * math.exp(-0.3 * i_layer)
        l = lamb.ap[:, j * 256:(j + 1) * 256]
        tmp = smalls[:, C_TMP:C_TMP + 128]
        emit(DVE, lambda: nc.vector.tensor_tensor(out=tmp[:, 0:64], in0=l[:, 0:64], in1=l[:, 64:128], op=ALU.mult),
             reads=[lamb], writes=[R_small])
        emit(DVE, lambda: nc.vector.tensor_tensor(out=tmp[:, 64:128], in0=l[:, 128:192], in1=l[:, 192:256], op=ALU.mult),
             reads=[lamb, R_small], writes=[R_small])
        s1 = smalls[:, C_TMP + 128:C_TMP + 129]
        s2 = smalls[:, C_TMP + 129:C_TMP + 130]
        emit(DVE, lambda: nc.vector.tensor_reduce(out=s1, in_=tmp[:, 0:64], axis=AX.X, op=ALU.add),
             reads=[R_small], writes=[R_small])
        emit(DVE, lambda: nc.vector.tensor_reduce(out=s2, in_=tmp[:, 64:128], axis=AX.X, op=ALU.add),
             reads=[R_small], writes=[R_small])
        emit(ACT, lambda: nc.scalar.activation(out=s1, in_=s1, func=AF.Exp), reads=[R_small], writes=[R_small])
        emit(ACT, lambda: nc.scalar.activation(out=s2, in_=s2, func=AF.Exp), reads=[R_small], writes=[R_small])
        nl = smalls[:, C_NLAM + j:C_NLAM + j + 1]
        emit(DVE, lambda: nc.vector.tensor_tensor(out=nl, in0=s2, in1=s1, op=ALU.subtract),
             reads=[R_small], writes=[R_small])
        emit(DVE, lambda: nc.vector.tensor_scalar(out=nl, in0=nl, scalar1=-lam_init, scalar2=None, op0=ALU.add),
             reads=[R_small], writes=[R_small])
        sw = smalls[:, C_SUBW + j:C_SUBW + j + 1]
        emit(DVE, lambda: nc.vector.tensor_scalar(out=sw, in0=sw, scalar1=1.0 - lam_init, scalar2=None, op0=ALU.mult),
             reads=[R_small], writes=[R_small])

    modrow = ar.f32(3 * D, "modrow", dma=True)
    adab = ar.f32(3 * D, "adab", dma=True)
    awt = [ar.f32(16 * 256, "awt%d" % k, dma=True) for k in range(2)]
    cnt = 0
    for i in range(DEPTH):
        dma(SP, adab.ap[0:2, :], dram_bcast(ada_b, i * 3 * D, 2, 3 * D), adab, writes=[adab])
        for ft in range(24):
            a = awt[cnt % 2]
            a3 = a.ap.rearrange("p (k n) -> p k n", k=16)
            dma(SP, a3, ada_w[i, :, ft * 256:(ft + 1) * 256].rearrange("(k p) n -> p k n", p=128), a, writes=[a])
            bk = cnt % 8
            begin(PE, reads=[a, R_small], writes=[BANK[bk]])
            for kc in range(16):
                ins = nc.tensor.matmul(bank_ap(bk)[0:2, 0:256], lhsT=smalls[:, C_SV + 2 * kc:C_SV + 2 * kc + 2],
                                       rhs=a3[:, kc, :], start=(kc == 0), stop=(kc == 15))
            end(PE, ins, reads=[a, R_small], writes=[BANK[bk]])
            emit(DVE, lambda: nc.vector.tensor_tensor(out=modrow.ap[0:2, ft * 256:(ft + 1) * 256],
                                                      in0=bank_ap(bk)[0:2, 0:256],
                                                      in1=adab.ap[0:2, ft * 256:(ft + 1) * 256], op=ALU.add),
                 reads=[BANK[bk], adab], writes=[modrow])
            cnt += 1
        emit(DVE, lambda: nc.vector.tensor_scalar(out=modrow.ap[0:2, D:2 * D], in0=modrow.ap[0:2, D:2 * D],
                                                  scalar1=1.0, scalar2=None, op0=ALU.add),
             reads=[modrow], writes=[modrow])
        dma(SP, bass.AP(modd, i * 2 * 3 * D, [[3 * D, 2], [1, 3 * D]]), modrow.ap[0:2, :], modrow,
            reads=[modrow], writes=[R_modd])
    kb.barrier()

    layer_ids = list(range(N_LAYERS_RUN))
    if os.environ.get('KLAYERS'):
        layer_ids = [int(c_) for c_ in os.environ['KLAYERS']]
    chunks = []
    for i in layer_ids:
        for g in groups:
            for c in range(16):
                chunks.append((i, c))
    issued = [0]

    def prefetch(n):
        while issued[0] <= n and issued[0] < len(chunks):
            k = issued[0]
            i, c = chunks[k]
            j = i // 2
            w = conv_w_in if i % 2 == 0 else attn_w_in
            s = k % 2
            for m in range(4):
                src = w[j, :, m * D + c * 128:m * D + (c + 1) * 128].rearrange("(k p) n -> p k n", p=128)
                dma(POOL, wbuf_t[s][:, m, :, :], src, R_wbuf[s], writes=[R_wbuf[s]])
            issued[0] += 1

    chunk_ctr = [0]
    rot = [0]

    def next_bank(lo=0, n=4):
        b = lo + rot[0] % n
        rot[0] += 1
        return b

    hT = bigA_t
    wo = bigA_t

    def load_wo(i):
        j = i // 2
        w = conv_w_out if i % 2 == 0 else attn_w_out
        for q4 in range(4):
            src = w[j, q4 * 512:(q4 + 1) * 512, :].rearrange("(k p) n -> p k n", p=128)
            dma(POOL, wo[:, q4 * 4:(q4 + 1) * 4, :], src, R_bigA, writes=[R_bigA])

    def phase_H(i, g, pos):
        T = g.T
        nt = T // 128
        ar = Arena(kb, arena_t, ARENA_WORDS)
        scb = ar.f32(D, "scb", dma=True)
        shb = ar.f32(D, "shb", dma=True)
        NX = 3
        xt = [ar.f32(D, "xt%d" % k, dma=True) for k in range(NX)]
        t1 = [ar.f32(D, "t1%d" % k) for k in range(2)]
        hb = [ar.bf16(D, "hb%d" % k) for k in range(2)]
        base = (i * 2 + g.idx) * 3 * D
        dma(SP, shb.ap, dram_bcast(modd, base, 128, D), shb, reads=[R_modd], writes=[shb])
        dma(SP, scb.ap, dram_bcast(modd, base + D, 128, D), scb, reads=[R_modd], writes=[scb])
        X, XR = g.X[pos], g.XR[pos]

        def load(tt):
            x = xt[tt % NX]
            dma(SP, x.ap, X[tt * 128:(tt + 1) * 128, :], x, reads=[XR], writes=[x])

        for tt in range(min(NX - 1, nt)):
            load(tt)
        for tt in range(nt):
            if tt + NX - 1 < nt:
                load(tt + NX - 1)
            x = xt[tt % NX]
            t = t1[tt % 2]
            emit(DVE, lambda: nc.vector.tensor_tensor(out=t.ap, in0=x.ap, in1=scb.ap, op=ALU.mult),
                 reads=[x, scb], writes=[t])
            h = hb[tt % 2]
            emit(POOL, lambda: nc.gpsimd.tensor_tensor(out=h.ap, in0=t.ap, in1=shb.ap, op=ALU.add),
                 reads=[t, shb], writes=[h])
            b0 = (tt % 4) * 2
            pbf = psum_t[:, b0 * 512:(b0 + 2) * 512].bitcast(BF16)
            begin(PE, reads=[h, R_cst], writes=[BANK[b0], BANK[b0 + 1]])
            for kc in range(16):
                ins = nc.tensor.transpose(out=pbf[:, kc * 128:(kc + 1) * 128], in_=h.ap[:, kc * 128:(kc + 1) * 128],
                                          identity=ident_bf)
            end(PE, ins, reads=[h, R_cst], writes=[BANK[b0], BANK[b0 + 1]])
            emit(ACT, lambda: nc.scalar.activation(out=hT[:, :, tt * 128:(tt + 1) * 128],
                                                   in_=pbf.rearrange("p (k n) -> p k n", k=16), func=AF.Copy),
                 reads=[BANK[b0], BANK[b0 + 1]], writes=[R_bigA])
        kb.barrier()

    def phase_P_conv(i, g):
        j = i // 2
        T, S, L = g.T, g.S, g.L
        ntq = T // 512
        ar = Arena(kb, arena_t, ARENA_WORDS)
        cu = ar.f32(S * (L + 2), "cu")
        bt = ar.f32(T, "bt")
        zt = ar.f32(T, "zt")
        cvt = ar.f32(T, "cv")
        ut = [ar.f32(512, "ut%d" % k) for k in range(2)]
        yb = [ar.bf16(T, "yb%d" % k, dma=True) for k in range(2)]
        cu3 = cu.ap.rearrange("p (s l) -> p s l", s=S)
        emit(DVE, lambda: nc.vector.memset(cu.ap, 0.0), writes=[cu])

        def seg(ap2d_or_3d_T, tq, is_cu):
            if L >= 512:
                s = (tq * 512) // L
                o = (tq * 512) % L
                if is_cu:
                    return cu3[:, s, 1 + o:1 + o + 512]
                return ap2d_or_3d_T[:, tq * 512:(tq + 1) * 512]
            else:
                ns = 512 // L
                if is_cu:
                    return cu3[:, tq * ns:(tq + 1) * ns, 1:L + 1]
                return ap2d_or_3d_T[:, tq * 512:(tq + 1) * 512].rearrange("p (s l) -> p s l", s=ns)

        def pview(bk):
            if L >= 512:
                return bank_ap(bk)
            return bank_ap(bk).rearrange("p (s l) -> p s l", s=512 // L)

        for fc in range(16):
            k = chunk_ctr[0]
            chunk_ctr[0] += 1
            s = k % 2
            prefetch(k + 1)
            wb = wbuf_t[s]
            for tq in range(ntq):
                bset = (tq % 2) * 4
                for m in (1, 2, 0, 3):
                    bk = bset + m
                    begin(PE, reads=[R_wbuf[s], R_bigA], writes=[BANK[bk]])
                    for kc in range(16):
                        ins = nc.tensor.matmul(bank_ap(bk), lhsT=wb[:, m, kc, :], rhs=hT[:, kc, tq * 512:(tq + 1) * 512],
                                               start=(kc == 0), stop=(kc == 15))
                    end(PE, ins, reads=[R_wbuf[s], R_bigA], writes=[BANK[bk]])
                if fc == 15 and tq == ntq - 1:
                    load_wo(i)
                u = ut[tq % 2]
                emit(ACT, lambda: nc.scalar.activation(out=u.ap, in_=bank_ap(bset + 2), func=AF.Copy),
                     reads=[BANK[bset + 2]], writes=[u])
                uv = u.ap if L >= 512 else u.ap.rearrange("p (s l) -> p s l", s=512 // L)
                emit(DVE, lambda: nc.vector.tensor_tensor(out=seg(None, tq, True), in0=pview(bset + 1), in1=uv,
                                                          op=ALU.mult),
                     reads=[BANK[bset + 1], u], writes=[cu])
                emit(ACT, lambda: nc.scalar.activation(out=bt.ap[:, tq * 512:(tq + 1) * 512], in_=bank_ap(bset + 0),
                                                       func=AF.Copy),
                     reads=[BANK[bset + 0]], writes=[bt])
                emit(ACT, lambda: nc.scalar.activation(out=zt.ap[:, tq * 512:(tq + 1) * 512], in_=bank_ap(bset + 3),
                                                       func=AF.Silu),
                     reads=[BANK[bset + 3]], writes=[zt])
            cw = lambda kk: convw_sb[:, (j * 16 + fc) * 3 + kk:(j * 16 + fc) * 3 + kk + 1]
            cv3 = cvt.ap.rearrange("p (s l) -> p s l", s=S)
            emit(DVE, lambda: nc.vector.tensor_scalar(out=cv3, in0=cu3[:, :, 1:L + 1], scalar1=cw(1), scalar2=None,
                                                      op0=ALU.mult),
                 reads=[cu, R_small], writes=[cvt])
            emit(DVE, lambda: nc.vector.scalar_tensor_tensor(out=cv3, in0=cu3[:, :, 0:L], scalar=cw(0), in1=cv3,
                                                             op0=ALU.mult, op1=ALU.add),
                 reads=[cu, cvt, R_small], writes=[cvt])
            emit(DVE, lambda: nc.vector.scalar_tensor_tensor(out=cv3, in0=cu3[:, :, 2:L + 2], scalar=cw(2), in1=cv3,
                                                             op0=ALU.mult, op1=ALU.add),
                 reads=[cu, cvt, R_small], writes=[cvt])
            emit(DVE, lambda: nc.vector.tensor_tensor(out=cvt.ap, in0=cvt.ap, in1=bt.ap, op=ALU.mult),
                 reads=[cvt, bt], writes=[cvt])
            y = yb[fc % 2]
            emit(DVE, lambda: nc.vector.tensor_tensor(out=y.ap, in0=cvt.ap, in1=zt.ap, op=ALU.mult),
                 reads=[cvt, zt], writes=[y])
            dma(SP, g.Y[fc, :, :], y.ap, y, reads=[y], writes=[g.YR])
        kb.barrier()

    def phase_P_attn(i, g):
        j = i // 2
        T, S, L = g.T, g.S, g.L
        ntq = T // 512
        nkeys = L + (PAST if g.cache else 0)
        nkt = nkeys // 128
        QC = min(512, L)
        nqc = L // QC
        ar = Arena(kb, arena_t, ARENA_WORDS)
        if g.cache:
            cosb = ar.f32(LS, "cos", dma=True)
            sinb = ar.f32(LS, "sin", dma=True)
            dma(SP, cosb.ap, cos_in[:, :], cosb, writes=[cosb])
            dma(SP, sinb.ap, sin_in[:, :], sinb, writes=[sinb])
            ckt = ar.bf16(4 * 128, "ckt", dma=True)
            xb = [ar.bf16(512, "xb%d" % k_) for k_ in range(2)]
            ra = ar.f32(512, "ra")
            rb = ar.f32(512, "rb")
        QT = ar.bf16(T, "QT")
        KT = ar.bf16(S * nkeys, "KT")
        Vb = ar.bf16(S * nkt * 128, "Vb", dma=True)
        Vb3 = Vb.ap.rearrange("p (t e) -> p t e", e=128)
        zs = ar.f32(T, "zs")
        PT = [ar.bf16(1024, "pt%d" % a) for a in range(2)]
        acc = [ar.f32(1024, "acc%d" % a) for a in range(2)]
        NU = 1 if g.cache else 2
        fo01 = [ar.f32(1024, "fo01%d" % u_) for u_ in range(NU)]
        fr = [ar.f32(1024, "fr%d" % u_) for u_ in range(NU)]
        fo = [ar.f32(512, "fo%d" % u_) for u_ in range(NU)]
        fsq = [ar.f32(512, "fsq%d" % u_) for u_ in range(NU)]
        frs = [ar.f32(512, "frs%d" % u_) for u_ in range(NU)]
        ystage = [ar.bf16(T, "yst%d" % k, dma=True) for k in range(2)]
        if not g.cache:
            kst = [ar.f32(T, "kst%d" % k, dma=True) for k in range(2)]
            vst = [ar.f32(T, "vst%d" % k, dma=True) for k in range(2)]
        nlam = smalls[:, C_NLAM + j:C_NLAM + j + 1]
        subw = smalls[:, C_SUBW + j:C_SUBW + j + 1]
        epsc = smalls[:, C_EPS:C_EPS + 1]
        OB0, OB1, SB0, SB1 = 4, 5, 6, 7

        for hd in range(16):
            k = chunk_ctr[0]
            chunk_ctr[0] += 1
            s = k % 2
            prefetch(k + 1)
            wb = wbuf_t[s]
            RW = R_wbuf[s]
            if g.cache:
                dma(POOL, ckt.ap.rearrange("p (t n) -> p t n", t=4),
                    ck_in[j, :, hd * 128:(hd + 1) * 128].rearrange("(t p) n -> p t n", p=128), ckt, writes=[ckt])
                dma(POOL, Vb3[:, 16:20, :],
                    cv_in[j, :, hd * 128:(hd + 1) * 128].rearrange("(t p) n -> p t n", p=128), Vb, writes=[Vb])
            pbanks = [0, 1, 2, 3, 6, 7]
            ropeq = []

            def rope_tail(item):
                bk, dv, tq, xbb = item
                bk2 = pbanks[rot[0] % 6]
                rot[0] += 1
                emit(PE, lambda: nc.tensor.matmul(bank_ap(bk2), lhsT=perm_bf, rhs=xbb.ap, start=True, stop=True),
                     reads=[xbb, R_cst], writes=[BANK[bk2]])
                emit(DVE, lambda: nc.vector.tensor_tensor(out=ra.ap, in0=bank_ap(bk),
                                                          in1=cosb.ap[:, tq * 512:(tq + 1) * 512], op=ALU.mult),
                     reads=[BANK[bk], cosb, xbb], writes=[ra])
                emit(DVE, lambda: nc.vector.tensor_tensor(out=rb.ap, in0=bank_ap(bk2),
                                                          in1=sinb.ap[:, tq * 512:(tq + 1) * 512], op=ALU.mult),
                     reads=[BANK[bk2], sinb], writes=[rb])
                emit(DVE, lambda: nc.vector.tensor_tensor(out=dv[0], in0=ra.ap, in1=rb.ap, op=ALU.add),
                     reads=[ra, rb], writes=[dv[1]])

            ngrp = 0
            for m in ((0, 1) if 'q' not in os.environ.get('KSKIP', '') else ()):
                dest = QT if m == 0 else KT
                for tq in range(ntq):
                    bk = pbanks[rot[0] % 6]
                    rot[0] += 1
                    begin(PE, reads=[RW, R_bigA], writes=[BANK[bk]])
                    for kc in range(16):
                        ins = nc.tensor.matmul(bank_ap(bk), lhsT=wb[:, m, kc, :], rhs=hT[:, kc, tq * 512:(tq + 1) * 512],
                                               start=(kc == 0), stop=(kc == 15))
                    end(PE, ins, reads=[RW, R_bigA], writes=[BANK[bk]])
                    dv = dest.ap[:, tq * 512:(tq + 1) * 512]
                    if not g.cache:
                        emit(ACT, lambda: nc.scalar.activation(out=dv, in_=bank_ap(bk), func=AF.Copy),
                             reads=[BANK[bk]], writes=[dest])
                    else:
                        xbb = xb[ngrp % 2]
                        ngrp += 1
                        emit(ACT, lambda: nc.scalar.activation(out=xbb.ap, in_=bank_ap(bk), func=AF.Copy),
                             reads=[BANK[bk]], writes=[xbb])
                        if ropeq:
                            rope_tail(ropeq.pop(0))
                        ropeq.append((bk, (dv, dest), tq, xbb))
            while ropeq:
                rope_tail(ropeq.pop(0))
            for tq in range(ntq if 'z' not in os.environ.get('KSKIP', '') else 0):
                bk = next_bank()
                begin(PE, reads=[RW, R_bigA], writes=[BANK[bk]])
                for kc in range(16):
                    ins = nc.tensor.matmul(bank_ap(bk), lhsT=wb[:, 3, kc, :], rhs=hT[:, kc, tq * 512:(tq + 1) * 512],
                                           start=(kc == 0), stop=(kc == 15))
                end(PE, ins, reads=[RW, R_bigA], writes=[BANK[bk]])
                emit(ACT, lambda: nc.scalar.activation(out=zs.ap[:, tq * 512:(tq + 1) * 512], in_=bank_ap(bk),
                                                       func=AF.Silu),
                     reads=[BANK[bk]], writes=[zs])
            mats = [2] if g.cache else [2, 1]
            if 'v' in os.environ.get('KSKIP', ''):
                mats = []
            for m in mats:
                for tt in range(T // 128):
                    bk = next_bank()
                    begin(PE, reads=[RW, R_bigA], writes=[BANK[bk]])
                    for kc in range(16):
                        ins = nc.tensor.matmul(bank_ap(bk, 128), lhsT=hT[:, kc, tt * 128:(tt + 1) * 128], rhs=wb[:, m, kc, :],
                                               start=(kc == 0), stop=(kc == 15))
                    end(PE, ins, reads=[RW, R_bigA], writes=[BANK[bk]])
                    tps = L // 128
                    vi = (tt // tps) * nkt + (tt % tps)
                    if not g.cache:
                        st = (vst if m == 2 else kst)[hd % 2]
                        emit(DVE, lambda: nc.vector.tensor_copy(out=st.ap[:, tt * 128:(tt + 1) * 128], in_=bank_ap(bk, 128)),
                             reads=[BANK[bk]], writes=[st])
                        if m == 2:
                            emit(ACT, lambda: nc.scalar.activation(out=Vb3[:, vi, :], in_=st.ap[:, tt * 128:(tt + 1) * 128], func=AF.Copy),
                                 reads=[st], writes=[Vb])
                    else:
                        emit(ACT, lambda: nc.scalar.activation(out=Vb3[:, vi, :], in_=bank_ap(bk, 128), func=AF.Copy),
                             reads=[BANK[bk]], writes=[Vb])
                if not g.cache and 'k' not in os.environ.get('KSKIP', ''):
                    st = (vst if m == 2 else kst)[hd % 2]
                    o_t = nv_out if m == 2 else nk_out
                    st4 = st.ap.rearrange("p (s th e) -> p s th e", s=S, e=128)
                    for s_i in range(S):
                        dst = o_t[s_i, j, :, hd * 128:(hd + 1) * 128].rearrange("(th p) e -> p th e", p=128)
                        dma(SP, dst, st4[:, s_i, :, :], st, reads=[st], writes=[])
            if hd == 15:
                load_wo(i)
            if g.cache:
                bk = next_bank()
                pbf = bank_ap(bk).bitcast(BF16)
                ck3 = ckt.ap.rearrange("p (t n) -> p t n", t=4)
                begin(PE, reads=[ckt, R_cst], writes=[BANK[bk]])
                for t in range(4):
                    ins = nc.tensor.transpose(out=pbf[:, t * 128:(t + 1) * 128], in_=ck3[:, t, :], identity=ident_bf)
                end(PE, ins, reads=[ckt, R_cst], writes=[BANK[bk]])
                emit(ACT, lambda: nc.scalar.activation(out=KT.ap[:, LS:LS + PAST], in_=pbf[:, 0:512], func=AF.Copy),
                     reads=[BANK[bk]], writes=[KT])
            yst = ystage[hd % 2]
            npair = 0
            nchunk = 0
            pending = []
            _skip = os.environ.get('KSKIP', '')
            defer = nkt >= 16 and os.environ.get('KDEFER', '0') == '1'

            def v3w(ap1024, c0, w):
                return ap1024.rearrange("p (c n) -> p c n", c=2)[:, :, c0:c0 + w]

            def make_finish(q0, W, accR, u, sbanks):
                f01, frR, foR, fsqR, frsR = fo01[u], fr[u], fo[u], fsq[u], frs[u]
                sa, sb_ = sbanks

                def st1():
                    for c_, SBk in enumerate((sa, sb_)):
                        emit(PE, lambda: nc.tensor.matmul(bank_ap(SBk, W), lhsT=ones32[:, :], rhs=accR.ap[:, c_ * 512:c_ * 512 + W],
                                                          start=True, stop=True),
                             reads=[accR, R_cst], writes=[BANK[SBk]])
                    PE.e.wait_ge(PE.sem, PE.t)

                def st2():
                    sv = psum_t[:, sa * 512:(sa + 2) * 512].rearrange("p (c n) -> p c n", c=2)[:, :, 0:W]
                    emit(ACT, lambda: nc.scalar.activation(out=v3w(frR.ap, 0, W), in_=sv, func=AF.Ln),
                         reads=[BANK[sa], BANK[sb_]], writes=[frR])
                    emit(ACT, lambda: nc.scalar.activation(out=v3w(frR.ap, 0, W), in_=v3w(frR.ap, 0, W), func=AF.Exp, scale=-1.0),
                         reads=[frR], writes=[frR])

                def st3():
                    emit(DVE, lambda: nc.vector.tensor_tensor(out=v3w(f01.ap, 0, W), in0=v3w(f01.ap, 0, W), in1=v3w(frR.ap, 0, W), op=ALU.mult),
                         reads=[f01, frR], writes=[f01])
                    emit(DVE, lambda: nc.vector.scalar_tensor_tensor(out=foR.ap[:, 0:W], in0=f01.ap[:, 512:512 + W], scalar=nlam,
                                                                     in1=f01.ap[:, 0:W], op0=ALU.mult, op1=ALU.add),
                         reads=[f01, R_small], writes=[foR])

                def st4():
                    emit(ACT, lambda: nc.scalar.activation(out=fsqR.ap[:, 0:W], in_=foR.ap[:, 0:W], func=AF.Square),
                         reads=[foR], writes=[fsqR])

                def st5():
                    emit(PE, lambda: nc.tensor.matmul(bank_ap(sa, W), lhsT=ones32[:, :], rhs=fsqR.ap[:, 0:W], start=True, stop=True),
                         reads=[fsqR, R_cst], writes=[BANK[sa]])
                    PE.e.wait_ge(PE.sem, PE.t)

                def st6():
                    emit(ACT, lambda: nc.scalar.activation(out=frsR.ap[:, 0:W], in_=bank_ap(sa, W), func=AF.Ln,
                                                           bias=epsc, scale=1.0 / 128.0),
                         reads=[BANK[sa], R_small], writes=[frsR])
                    emit(ACT, lambda: nc.scalar.activation(out=frsR.ap[:, 0:W], in_=frsR.ap[:, 0:W], func=AF.Exp, scale=-0.5),
                         reads=[frsR], writes=[frsR])

                def st7():
                    emit(DVE, lambda: nc.vector.scalar_tensor_tensor(out=foR.ap[:, 0:W], in0=foR.ap[:, 0:W], scalar=subw,
                                                                     in1=frsR.ap[:, 0:W], op0=ALU.mult, op1=ALU.mult),
                         reads=[foR, frsR, R_small], writes=[foR])
                    emit(DVE, lambda: nc.vector.tensor_tensor(out=yst.ap[:, q0:q0 + W], in0=foR.ap[:, 0:W],
                                                              in1=zs.ap[:, q0:q0 + W], op=ALU.mult),
                         reads=[foR, zs], writes=[yst])
                return [st1, st2, st3, st4, st5, st6, st7]

            for sq in range(S if 'a' not in _skip else 0):
                for qc in range(nqc):
                    q0 = sq * L + qc * QC
                    qv = QT.ap[:, q0:q0 + QC]
                    if g.cache:
                        u, c0 = 0, 0
                        accb = acc[nchunk % 2]
                    else:
                        u, c0 = sq // 2, (sq % 2) * QC
                        accb = acc[u]
                    nchunk += 1

                    def s_mm(kt, pair):
                        ka = KT.ap[:, sq * nkeys + kt * 128:sq * nkeys + (kt + 1) * 128]
                        ba, bb = 2 * pair, 2 * pair + 1
                        begin(PE, reads=[KT, QT], writes=[BANK[ba], BANK[bb]])
                        nc.tensor.matmul(bank_ap(ba, QC), lhsT=ka[0:64, :], rhs=qv[0:64, :], start=True, stop=True)
                        ins = nc.tensor.matmul(bank_ap(bb, QC), lhsT=ka[64:128, :], rhs=qv[64:128, :], start=True, stop=True)
                        end(PE, ins, reads=[KT, QT], writes=[BANK[ba], BANK[bb]])

                    s_mm(0, npair % 2)
                    for kt in range(nkt):
                        pair = npair % 2
                        npair += 1
                        if kt + 1 < nkt:
                            s_mm(kt + 1, npair % 2)
                        ba, bb = 2 * pair, 2 * pair + 1
                        pt = PT[pair]
                        sv = psum_t[:, ba * 512:(ba + 2) * 512].rearrange("p (c n) -> p c n", c=2)[:, :, 0:QC]
                        emit(ACT, lambda: nc.scalar.activation(out=v3w(pt.ap, 0, QC), in_=sv, func=AF.Exp, scale=0.125),
                             reads=[BANK[ba], BANK[bb]], writes=[pt])
                        vt = Vb3[:, sq * nkt + kt, :]
                        first, last = (kt == 0), (kt == nkt - 1)
                        begin(PE, reads=[pt, Vb], writes=[BANK[OB0], BANK[OB1]])
                        nc.tensor.matmul(bank_ap(OB0, QC), lhsT=vt, rhs=pt.ap[:, 0:QC], start=first, stop=last)
                        ins = nc.tensor.matmul(bank_ap(OB1, QC), lhsT=vt, rhs=pt.ap[:, 512:512 + QC], start=first, stop=last)
                        end(PE, ins, reads=[pt, Vb], writes=[BANK[OB0], BANK[OB1]])
                        if first:
                            emit(DVE, lambda: nc.vector.tensor_copy(out=v3w(accb.ap, c0, QC), in_=v3w(pt.ap, 0, QC)),
                                 reads=[pt], writes=[accb])
                        else:
                            emit(DVE, lambda: nc.vector.tensor_tensor(out=v3w(accb.ap, c0, QC), in0=v3w(accb.ap, c0, QC),
                                                                      in1=v3w(pt.ap, 0, QC), op=ALU.add),
                                 reads=[pt, accb], writes=[accb])
                        if defer and pending and kt % 2 == 1:
                            pending.pop(0)()
                    while pending:
                        pending.pop(0)()
                    ov = psum_t[:, OB0 * 512:(OB0 + 2) * 512].rearrange("p (c n) -> p c n", c=2)[:, :, 0:QC]
                    emit(ACT, lambda: nc.scalar.activation(out=v3w(fo01[u].ap, c0, QC), in_=ov, func=AF.Copy),
                         reads=[BANK[OB0], BANK[OB1]], writes=[fo01[u]])
                    if g.cache:
                        pending = make_finish(q0, QC, accb, 0, (SB0, SB1))
                        if not defer:
                            while pending:
                                pending.pop(0)()
            while pending:
                pending.pop(0)()
            if not g.cache and 'a' not in _skip:
                fa = make_finish(0, 512, acc[0], 0, (SB0, SB1))
                fb = make_finish(512, 512, acc[1], 1, (2, 3))
                for sa_, sb2 in zip(fa, fb):
                    sa_()
                    sb2()
            dma(SP, g.Y[hd, :, :], yst.ap, yst, reads=[yst], writes=[g.YR])
        kb.barrier()

    def phase_O(i, g, pos):
        T = g.T
        nt = T // 128
        ar = Arena(kb, arena_t, ARENA_WORDS)
        gb = ar.f32(D, "gb", dma=True)
        gamb = ar.f32(D, "gamb", dma=True)
        betb = ar.f32(D, "betb", dma=True)
        xt = [ar.f32(D, "xt%d" % k, dma=True) for k in range(2)]
        yt = [ar.bf16(16 * 256, "yt%d" % k, dma=True) for k in range(2)]
        tv = [ar.f32(D, "tv%d" % k) for k in range(2)]
        xo = [ar.f32(D, "xo%d" % k, dma=True) for k in range(2)]
        stt = [ar.f32(32, "stt%d" % k) for k in range(2)]
        base = (i * 2 + g.idx) * 3 * D
        dma(SP, gb.ap, dram_bcast(modd, base + 2 * D, 128, D), gb, reads=[R_modd], writes=[gb])
        dma(SP, gamb.ap, dram_bcast(ln_g, i * D, 128, D), gamb, writes=[gamb])
        dma(SP, betb.ap, dram_bcast(ln_b, i * D, 128, D), betb, writes=[betb])
        X, XR = g.X[pos], g.XR[pos]
        XO, XOR = g.X[pos + 1], g.XR[pos + 1]
        epsc = smalls[:, C_EPS:C_EPS + 1]

        def load_y(pr):
            if pr * 2 >= nt:
                return
            y = yt[pr % 2]
            y3 = y.ap.rearrange("p (k t) -> p k t", k=16)
            dma(SP, y3, g.Y[:, :, pr * 256:pr * 256 + 256].rearrange("k p t -> p k t"), y, reads=[g.YR], writes=[y])

        def load(tt):
            x = xt[tt % 2]
            dma(SP, x.ap, X[tt * 128:(tt + 1) * 128, :], x, reads=[XR], writes=[x])

        def stage1(tt):
            y = yt[(tt // 2) % 2]
            y3 = y.ap.rearrange("p (k t) -> p k t", k=16)
            x = xt[tt % 2]
            t_ = tv[tt % 2]
            st = stt[tt % 2]
            bset = (tt % 2) * 4
            for dt in range(4):
                bk = bset + dt
                begin(PE, reads=[y, R_bigA], writes=[BANK[bk]])
                for kc in range(16):
                    ins = nc.tensor.matmul(bank_ap(bk), lhsT=y3[:, kc, (tt % 2) * 128:(tt % 2 + 1) * 128],
                                           rhs=wo[:, kc, dt * 512:(dt + 1) * 512], start=(kc == 0), stop=(kc == 15))
                end(PE, ins, reads=[y, R_bigA], writes=[BANK[bk]])
                emit(DVE, lambda: nc.vector.tensor_tensor(out=t_.ap[:, dt * 512:(dt + 1) * 512], in0=bank_ap(bk),
                                                          in1=gb.ap[:, dt * 512:(dt + 1) * 512], op=ALU.mult),
                     reads=[BANK[bk], gb], writes=[t_])
            emit(DVE, lambda: nc.vector.scalar_tensor_tensor(out=t_.ap, in0=x.ap, scalar=ALPHA, in1=t_.ap,
                                                             op0=ALU.mult, op1=ALU.add),
                 reads=[x, t_], writes=[t_])
            st6 = st.ap[:, 0:24].rearrange("p (c s) -> p c s", c=4)
            begin(DVE, reads=[t_], writes=[st])
            for c4 in range(4):
                ins = nc.vector.bn_stats(out=st6[:, c4, :], in_=t_.ap[:, c4 * 512:(c4 + 1) * 512])
            end(DVE, ins, reads=[t_], writes=[st])
            emit(DVE, lambda: nc.vector.bn_aggr(out=st.ap[:, 24:26], in_=st.ap[:, 0:24]), reads=[st], writes=[st])
            emit(ACT, lambda: nc.scalar.activation(out=st.ap[:, 26:27], in_=st.ap[:, 25:26], func=AF.Sqrt, bias=epsc, scale=1.0),
                 reads=[st, R_small], writes=[st])

        def stage2(tt):
            t_ = tv[tt % 2]
            st = stt[tt % 2]
            o = xo[tt % 2]
            rstd = st.ap[:, 27:28]
            nmr = st.ap[:, 28:29]
            emit(DVE, lambda: nc.vector.reciprocal(out=rstd, in_=st.ap[:, 26:27]), reads=[st], writes=[st])
            emit(DVE, lambda: nc.vector.scalar_tensor_tensor(out=nmr, in0=st.ap[:, 24:25], scalar=-1.0, in1=rstd,
                                                             op0=ALU.mult, op1=ALU.mult),
                 reads=[st], writes=[st])
            emit(ACT, lambda: nc.scalar.activation(out=o.ap, in_=t_.ap, func=AF.Identity, bias=nmr, scale=rstd),
                 reads=[t_, st], writes=[o])
            emit(POOL, lambda: nc.gpsimd.tensor_tensor(out=o.ap, in0=o.ap, in1=gamb.ap, op=ALU.mult),
                 reads=[o, gamb], writes=[o])
            emit(POOL, lambda: nc.gpsimd.tensor_tensor(out=o.ap, in0=o.ap, in1=betb.ap, op=ALU.add),
                 reads=[o, betb], writes=[o])
            dma(SP, XO[tt * 128:(tt + 1) * 128, :], o.ap, o, reads=[o], writes=[XOR])

        load_y(0)
        load(0)
        for tt in range(nt):
            if tt % 2 == 0:
                load_y(tt // 2 + 1)
            if tt + 1 < nt:
                load(tt + 1)
            stage1(tt)
            if tt >= 1:
                stage2(tt - 1)
        stage2(nt - 1)
        kb.barrier()

    prefetch(0)
    for pos, i in enumerate(layer_ids):
        for g in groups:
            phase_H(i, g, pos)
            if i % 2 == 0:
                phase_P_conv(i, g)
            else:
                phase_P_attn(i, g)
            phase_O(i, g, pos)
    kb.barrier()
    return nc


_NC_CACHE = {}


def _rope_tables():
    p = np.arange(128)
    axis = (p % 64) // 32
    half = (p % 32) // 16
    jj = p % 16
    inv = (np.float32(10000.0) ** (-(np.arange(16, dtype=np.float32)) / np.float32(16))).astype(np.float32)
    t = np.arange(LS)
    row = (t // 64).astype(np.float32)
    col = (t % 64).astype(np.float32)
    pos = np.where(axis[:, None] == 0, row[None, :], col[None, :]).astype(np.float32)
    ang = (pos * inv[jj][:, None]).astype(np.float32)
    cosT = np.cos(ang).astype(np.float32)
    sgn = np.where(half == 0, -1.0, 1.0).astype(np.float32)
    sinT = (np.sin(ang).astype(np.float32) * sgn[:, None]).astype(np.float32)
    partner = np.where(half == 0, p + 16, p - 16)
    perm = np.zeros((128, 128), np.float32)
    perm[partner, p] = 1.0
    return cosT, sinT, perm


def make_in_maps(inputs, cores):
    f = lambda a: np.ascontiguousarray(np.asarray(a, dtype=np.float32))
    x_prompt, x_sample = f(inputs["x_prompt"]), f(inputs["x_sample"])
    cache_k, cache_v = f(inputs["cache_k"]), f(inputs["cache_v"])
    c, c_ctx = f(inputs["c"]), f(inputs["c_ctx"])
    cosT, sinT, perm = _rope_tables()
    cmat = np.concatenate([np.eye(128, dtype=np.float32), perm, np.ones((128, 128), np.float32)], axis=1)
    conv_w = f(inputs["conv_w"])
    conv_wT = np.ascontiguousarray(conv_w.reshape(2, 3, 16, 128).transpose(3, 0, 2, 1).reshape(128, 96))
    subwT = np.ascontiguousarray(f(inputs["attn_subln_w"]).T)
    shared = {
        "ada_w": f(inputs["ada_w"]), "ada_b": f(inputs["ada_b"]), "ln_g": f(inputs["ln_g"]), "ln_b": f(inputs["ln_b"]),
        "conv_w_in": f(inputs["conv_w_in"]), "conv_wT": conv_wT, "conv_w_out": f(inputs["conv_w_out"]),
        "attn_w_in": f(inputs["attn_w_in"]), "attn_lam": f(inputs["attn_lambda"]).reshape(2, 256),
        "subwT": subwT, "attn_w_out": f(inputs["attn_w_out"]), "cosT": cosT, "sinT": sinT, "cmat": cmat,
    }
    maps = []
    for b in cores:
        sv = np.stack([c_ctx, c[b]], axis=0)
        svT = np.ascontiguousarray(sv.reshape(2, 16, 128).transpose(2, 1, 0).reshape(128, 32))
        m = dict(shared)
        m["xs"] = x_sample[b]
        m["xp"] = np.ascontiguousarray(x_prompt[4 * b:4 * b + 4].reshape(SP_PER_CORE * LP, D))
        m["ck"] = np.ascontiguousarray(cache_k[b].reshape(2, PAST, D))
        m["cv"] = np.ascontiguousarray(cache_v[b].reshape(2, PAST, D))
        m["svT"] = svT
        maps.append(m)
    return maps


def kernel(**inputs):
    if "nc" not in _NC_CACHE:
        _NC_CACHE["nc"] = build_program()
    nc = _NC_CACHE["nc"]
    cores = list(range(NCORES))
    in_maps = make_in_maps(inputs, cores)
    res = run_bass_kernel_spmd(nc, in_maps, core_ids=cores)
    r = res.results
    y_prompt = np.concatenate([r[b]["yp"].reshape(SP_PER_CORE, LP, D) for b in cores], axis=0).astype(np.float32)
    y_sample = np.stack([r[b]["ys"] for b in cores], axis=0).astype(np.float32)
    nk = np.concatenate([r[b]["nk"].reshape(SP_PER_CORE, 2, LP, 16, 2, 64) for b in cores], axis=0).astype(np.float32)
    nv = np.concatenate([r[b]["nv"].reshape(SP_PER_CORE, 2, LP, 16, 128) for b in cores], axis=0).astype(np.float32)
    return (y_prompt, y_sample, nk, nv)
```

```python
import math
import os
import numpy as np
import concourse.bass as bass
import concourse.mybir as mybir
from concourse.bass_utils import run_bass_kernel_spmd

F32 = mybir.dt.float32
BF16 = mybir.dt.bfloat16
AF = mybir.ActivationFunctionType
ALU = mybir.AluOpType
AX = mybir.AxisListType

D = 2048
DEPTH = 4
NCORES = 8
LS = 2048
LP = 256
SP_PER_CORE = 4
PAST = 512
ALPHA = (2.0 * DEPTH) ** 0.25
LN_EPS = 1e-5
SELF_SYNC = True
N_LAYERS_RUN = DEPTH


class Eng:
    def __init__(self, nc, e, name):
        self.e = e
        self.name = name
        self.sem = nc.alloc_semaphore("tk_" + name)
        self.t = 0
        self.seen = {}

    def wait(self, dep):
        sem, val, key = dep
        if key == self.name and (self.name == "pe" or not SELF_SYNC):
            return
        if self.seen.get(key, 0) >= val:
            return
        self.seen[key] = val
        self.e.wait_ge(sem, val)


class DSem:
    def __init__(self, nc, name):
        self.handle = nc.alloc_semaphore(name)
        self.count = 0
        self.name = name


class Res:
    def __init__(self, name, ap=None, ds=None):
        self.name = name
        self.ap = ap
        self.ds = ds
        self.wr = {}
        self.rd = {}


class Group:
    def __init__(self, idx, S, L, cache):
        self.idx = idx
        self.S = S
        self.L = L
        self.T = S * L
        self.cache = cache


class KB:
    def __init__(self, nc):
        self.nc = nc
        self.PE = Eng(nc, nc.tensor, "pe")
        self.ACT = Eng(nc, nc.scalar, "act")
        self.DVE = Eng(nc, nc.vector, "dve")
        self.POOL = Eng(nc, nc.gpsimd, "pool")
        self.SP = Eng(nc, nc.sync, "sp")
        self.engs = [self.PE, self.ACT, self.DVE, self.POOL, self.SP]
        self.free_ds = []
        self.borrowed = []
        self.nds = 0

    def _waits(self, eng, reads, writes):
        for r in reads:
            for d in list(r.wr.values()):
                eng.wait(d)
        for w in writes:
            for d in list(w.wr.values()):
                eng.wait(d)
            for d in list(w.rd.values()):
                eng.wait(d)

    def begin(self, eng, reads=(), writes=()):
        self._waits(eng, reads, writes)

    def end(self, eng, ins, reads=(), writes=()):
        eng.t += 1
        ins.then_inc(eng.sem, 1)
        d = (eng.sem, eng.t, eng.name)
        for r in reads:
            r.rd[eng.name] = d
        for w in writes:
            w.wr[eng.name] = d

    def emit(self, eng, mk, reads=(), writes=()):
        self._waits(eng, reads, writes)
        ins = mk()
        self.end(eng, ins, reads, writes)
        return ins

    def dma(self, q, out, in_, slot, reads=(), writes=()):
        self._waits(q, reads, writes)
        ins = q.e.dma_start(out=out, in_=in_)
        ds = slot.ds
        ds.count += 16
        ins.then_inc(ds.handle, 16)
        d = (ds.handle, ds.count, ds.name)
        for r in reads:
            r.rd[ds.name] = d
        for w in writes:
            w.wr[ds.name] = d

    def borrow_ds(self):
        if self.free_ds:
            ds = self.free_ds.pop()
        else:
            ds = DSem(self.nc, "ds%d" % self.nds)
            self.nds += 1
        self.borrowed.append(ds)
        return ds

    def barrier(self):
        for E in self.engs:
            for Fg in self.engs:
                if Fg is not E and Fg.t > 0:
                    E.wait((Fg.sem, Fg.t, Fg.name))
            for ds in self.borrowed:
                if ds.count > 0:
                    E.wait((ds.handle, ds.count, ds.name))
        self.free_ds.extend(self.borrowed)
        self.borrowed = []


class Arena:
    def __init__(self, kb, tensor, nwords):
        self.kb = kb
        self.t = tensor
        self.n = nwords
        self.off = 0

    def f32(self, n, name, dma=False):
        n2 = (n + 7) // 8 * 8
        assert self.off + n2 <= self.n, ("arena overflow", name, self.off, n2, self.n)
        ap = self.t[:, self.off:self.off + n]
        self.off += n2
        return Res(name, ap, self.kb.borrow_ds() if dma else None)

    def bf16(self, n, name, dma=False):
        nw = (n + 1) // 2
        n2 = (nw + 7) // 8 * 8
        assert self.off + n2 <= self.n, ("arena overflow", name, self.off, n2, self.n)
        ap = self.t[:, self.off:self.off + nw].bitcast(BF16)
        self.off += n2
        return Res(name, ap, self.kb.borrow_ds() if dma else None)


def dram_bcast(handle, offset, nparts, n):
    return bass.AP(handle, offset, [[0, nparts], [1, n]])


def build_program():
    nc = bass.Bass("TRN2", target_bir_lowering=False)
    kb = KB(nc)
    PE, ACT, DVE, POOL, SP = kb.PE, kb.ACT, kb.DVE, kb.POOL, kb.SP
    emit, begin, end, dma = kb.emit, kb.begin, kb.end, kb.dma

    xs_in = nc.dram_tensor("xs", [LS, D], F32, kind="ExternalInput")
    xp_in = nc.dram_tensor("xp", [SP_PER_CORE * LP, D], F32, kind="ExternalInput")
    ck_in = nc.dram_tensor("ck", [2, PAST, D], F32, kind="ExternalInput")
    cv_in = nc.dram_tensor("cv", [2, PAST, D], F32, kind="ExternalInput")
    svT_in = nc.dram_tensor("svT", [128, 32], F32, kind="ExternalInput")
    ada_w = nc.dram_tensor("ada_w", [DEPTH, D, 3 * D], F32, kind="ExternalInput")
    ada_b = nc.dram_tensor("ada_b", [DEPTH, 3 * D], F32, kind="ExternalInput")
    ln_g = nc.dram_tensor("ln_g", [DEPTH, D], F32, kind="ExternalInput")
    ln_b = nc.dram_tensor("ln_b", [DEPTH, D], F32, kind="ExternalInput")
    conv_w_in = nc.dram_tensor("conv_w_in", [2, D, 4 * D], F32, kind="ExternalInput")
    conv_wT = nc.dram_tensor("conv_wT", [128, 2 * 16 * 3], F32, kind="ExternalInput")
    conv_w_out = nc.dram_tensor("conv_w_out", [2, D, D], F32, kind="ExternalInput")
    attn_w_in = nc.dram_tensor("attn_w_in", [2, D, 4 * D], F32, kind="ExternalInput")
    attn_lam = nc.dram_tensor("attn_lam", [2, 256], F32, kind="ExternalInput")
    subwT_in = nc.dram_tensor("subwT", [128, 2], F32, kind="ExternalInput")
    attn_w_out = nc.dram_tensor("attn_w_out", [2, D, D], F32, kind="ExternalInput")
    cos_in = nc.dram_tensor("cosT", [128, LS], F32, kind="ExternalInput")
    sin_in = nc.dram_tensor("sinT", [128, LS], F32, kind="ExternalInput")
    cmat_in = nc.dram_tensor("cmat", [128, 3 * 128], F32, kind="ExternalInput")

    ys_out = nc.dram_tensor("ys", [LS, D], F32, kind="ExternalOutput")
    yp_out = nc.dram_tensor("yp", [SP_PER_CORE * LP, D], F32, kind="ExternalOutput")
    nk_out = nc.dram_tensor("nk", [SP_PER_CORE, 2, LP, D], F32, kind="ExternalOutput")
    nv_out = nc.dram_tensor("nv", [SP_PER_CORE, 2, LP, D], F32, kind="ExternalOutput")

    xa_s = nc.dram_tensor("xa_s", [LS, D], F32)
    xb_s = nc.dram_tensor("xb_s", [LS, D], F32)
    xa_p = nc.dram_tensor("xa_p", [SP_PER_CORE * LP, D], F32)
    xb_p = nc.dram_tensor("xb_p", [SP_PER_CORE * LP, D], F32)
    y_s = nc.dram_tensor("y_s", [16, 128, LS], BF16)
    y_p = nc.dram_tensor("y_p", [16, 128, SP_PER_CORE * LP], BF16)
    modd = nc.dram_tensor("modd", [DEPTH * 2 * 3 * D], F32)

    R_modd = Res("modd")
    gP = Group(0, SP_PER_CORE, LP, False)
    gS = Group(1, 1, LS, True)
    for g_, xin, xa, xb, xout in ((gS, xs_in, xa_s, xb_s, ys_out), (gP, xp_in, xa_p, xb_p, yp_out)):
        ra_, rb_ = Res("xa"), Res("xb")
        g_.X = [xin] + [(xa, xb)[k % 2] for k in range(N_LAYERS_RUN - 1)] + [xout]
        g_.XR = [Res("xin")] + [(ra_, rb_)[k % 2] for k in range(N_LAYERS_RUN - 1)] + [Res("xout")]
    gS.Y, gP.Y = y_s, y_p
    gS.YR, gP.YR = Res("ys_y"), Res("yp_y")
    import os
    _kg = os.environ.get('KGROUPS', 'sp')
    groups = [g_ for g_, c_ in ((gS, 's'), (gP, 'p')) if c_ in _kg]

    bigA_t = nc.alloc_sbuf_tensor("bigA", [128, 16, 2048], BF16)
    R_bigA = Res("bigA", None, DSem(nc, "ds_bigA"))
    wbuf_t = [nc.alloc_sbuf_tensor("wbuf%d" % s, [128, 4, 16, 128], BF16) for s in range(2)]
    R_wbuf = [Res("wbuf%d" % s, None, DSem(nc, "ds_wbuf%d" % s)) for s in range(2)]
    ARENA_WORDS = 24576
    arena_t = nc.alloc_sbuf_tensor("arena", [128, ARENA_WORDS], F32)
    cst_bf = nc.alloc_sbuf_tensor("cst_bf", [128, 3 * 128], BF16)
    ones32 = nc.alloc_sbuf_tensor("ones32", [128, 128], F32)
    smalls = nc.alloc_sbuf_tensor("smalls", [128, 256], F32)
    convw_sb = nc.alloc_sbuf_tensor("convw_sb", [128, 96], F32)
    R_cst = Res("cst", None, DSem(nc, "ds_cst"))
    R_small = Res("smalls", None, DSem(nc, "ds_small"))
    ident_bf = cst_bf[:, 0:128]
    perm_bf = cst_bf[:, 128:256]
    ones_bf = cst_bf[:, 256:384]
    C_EPS = 0
    C_SUBW = 2
    C_NLAM = 4
    C_SV = 32
    C_TMP = 64
    psum_t = nc.alloc_psum_tensor("ps", [128, 4096], F32)
    BANK = [Res("bank%d" % k) for k in range(8)]

    def bank_ap(k, n=512):
        return psum_t[:, k * 512:k * 512 + n]

    dma(POOL, cst_bf[:, :], cmat_in[:, :], R_cst, writes=[R_cst])
    dma(SP, smalls[:, C_SV:C_SV + 32], svT_in[:, :], R_small, writes=[R_small])
    dma(SP, convw_sb[:, :], conv_wT[:, :], R_small, writes=[R_small])
    dma(SP, smalls[:, C_SUBW:C_SUBW + 2], subwT_in[:, :], R_small, writes=[R_small])
    emit(DVE, lambda: nc.vector.memset(ones32[:, :], 1.0), writes=[R_cst])
    emit(DVE, lambda: nc.vector.memset(smalls[:, C_EPS:C_EPS + 1], LN_EPS), writes=[R_small])
    emit(ACT, lambda: nc.scalar.activation(out=smalls[:, C_SV:C_SV + 32], in_=smalls[:, C_SV:C_SV + 32],
                                           func=AF.Silu), reads=[R_small], writes=[R_small])
    ar = Arena(kb, arena_t, ARENA_WORDS)
    lamb = ar.f32(512, "lamb", dma=True)
    dma(SP, lamb.ap, dram_bcast(attn_lam, 0, 128, 512), lamb, writes=[lamb])
    for j in range(2):
        i_layer = 2 * j + 1
        lam_init = 0.8 - 0.6 * math.exp(-0.3 * i_layer)
        l = lamb.ap[:, j * 256:(j + 1) * 256]
        tmp = smalls[:, C_TMP:C_TMP + 128]
        emit(DVE, lambda: nc.vector.tensor_tensor(out=tmp[:, 0:64], in0=l[:, 0:64], in1=l[:, 64:128], op=ALU.mult),
             reads=[lamb], writes=[R_small])
        emit(DVE, lambda: nc.vector.tensor_tensor(out=tmp[:, 64:128], in0=l[:, 128:192], in1=l[:, 192:256], op=ALU.mult),
             reads=[lamb, R_small], writes=[R_small])
        s1 = smalls[:, C_TMP + 128:C_TMP + 129]
        s2 = smalls[:, C_TMP + 129:C_TMP + 130]
        emit(DVE, lambda: nc.vector.tensor_reduce(out=s1, in_=tmp[:, 0:64], axis=AX.X, op=ALU.add),
             reads=[R_small], writes=[R_small])
        emit(DVE, lambda: nc.vector.tensor_reduce(out=s2, in_=tmp[:, 64:128], axis=AX.X, op=ALU.add),
             reads=[R_small], writes=[R_small])
        emit(ACT, lambda: nc.scalar.activation(out=s1, in_=s1, func=AF.Exp), reads=[R_small], writes=[R_small])
        emit(ACT, lambda: nc.scalar.activation(out=s2, in_=s2, func=AF.Exp), reads=[R_small], writes=[R_small])
        nl = smalls[:, C_NLAM + j:C_NLAM + j + 1]
        emit(DVE, lambda: nc.vector.tensor_tensor(out=nl, in0=s2, in1=s1, op=ALU.subtract),
             reads=[R_small], writes=[R_small])
        emit(DVE, lambda: nc.vector.tensor_scalar(out=nl, in0=nl, scalar1=-lam_init, scalar2=None, op0=ALU.add),
             reads=[R_small], writes=[R_small])
        sw = smalls[:, C_SUBW + j:C_SUBW + j + 1]
        emit(DVE, lambda: nc.vector.tensor_scalar(out=sw, in0=sw, scalar1=1.0 - lam_init, scalar2=None, op0=ALU.mult),
             reads=[R_small], writes=[R_small])

    modrow = ar.f32(3 * D, "modrow", dma=True)
    adab = ar.f32(3 * D, "adab", dma=True)
    awt = [ar.bf16(16 * 512, "awt%d" % k, dma=True) for k in range(2)]
    svb = ar.bf16(32, "svb")
    emit(DVE, lambda: nc.vector.tensor_copy(out=svb.ap, in_=smalls[:, C_SV:C_SV + 32]), reads=[R_small], writes=[svb])
    cnt = 0
    for i in range(DEPTH):
        dma(SP, adab.ap[0:2, :], dram_bcast(ada_b, i * 3 * D, 2, 3 * D), adab, writes=[adab])
        for ft in range(12):
            a = awt[cnt % 2]
            a3 = a.ap.rearrange("p (k n) -> p k n", k=16)
            dma(POOL, a3, ada_w[i, :, ft * 512:(ft + 1) * 512].rearrange("(k p) n -> p k n", p=128), a, writes=[a])
            bk = cnt % 8
            begin(PE, reads=[a, svb], writes=[BANK[bk]])
            for kc in range(16):
                ins = nc.tensor.matmul(bank_ap(bk)[0:2, :], lhsT=svb.ap[:, 2 * kc:2 * kc + 2],
                                       rhs=a3[:, kc, :], start=(kc == 0), stop=(kc == 15))
            end(PE, ins, reads=[a, svb], writes=[BANK[bk]])
            emit(DVE, lambda: nc.vector.tensor_tensor(out=modrow.ap[0:2, ft * 512:(ft + 1) * 512],
                                                      in0=bank_ap(bk)[0:2, :],
                                                      in1=adab.ap[0:2, ft * 512:(ft + 1) * 512], op=ALU.add),
                 reads=[BANK[bk], adab], writes=[modrow])
            cnt += 1
        emit(DVE, lambda: nc.vector.tensor_scalar(out=modrow.ap[0:2, D:2 * D], in0=modrow.ap[0:2, D:2 * D],
                                                  scalar1=1.0, scalar2=None, op0=ALU.add),
             reads=[modrow], writes=[modrow])
        dma(SP, bass.AP(modd, i * 2 * 3 * D, [[3 * D, 2], [1, 3 * D]]), modrow.ap[0:2, :], modrow,
            reads=[modrow], writes=[R_modd])
    kb.barrier()

    layer_ids = list(range(N_LAYERS_RUN))
    if os.environ.get('KLAYERS'):
        layer_ids = [int(c_) for c_ in os.environ['KLAYERS']]
    chunks = []
    for i in layer_ids:
        for g in groups:
            for c in range(16):
                chunks.append((i, c))
    issued = [0]

    def prefetch(n):
        while issued[0] <= n and issued[0] < len(chunks):
            k = issued[0]
            i, c = chunks[k]
            j = i // 2
            w = conv_w_in if i % 2 == 0 else attn_w_in
            s = k % 2
            for m in range(4):
                src = w[j, :, m * D + c * 128:m * D + (c + 1) * 128].rearrange("(k p) n -> p k n", p=128)
                dma(POOL, wbuf_t[s][:, m, :, :], src, R_wbuf[s], writes=[R_wbuf[s]])
            issued[0] += 1

    chunk_ctr = [0]
    rot = [0]

    def next_bank(lo=0, n=4):
        b = lo + rot[0] % n
        rot[0] += 1
        return b

    hT = bigA_t
    wo = bigA_t

    def load_wo(i):
        j = i // 2
        w = conv_w_out if i % 2 == 0 else attn_w_out
        for q4 in range(4):
            src = w[j, q4 * 512:(q4 + 1) * 512, :].rearrange("(k p) n -> p k n", p=128)
            dma(POOL, wo[:, q4 * 4:(q4 + 1) * 4, :], src, R_bigA, writes=[R_bigA])

    def phase_H(i, g, pos):
        T = g.T
        nt = T // 128
        ar = Arena(kb, arena_t, ARENA_WORDS)
        scb = ar.f32(D, "scb", dma=True)
        shb = ar.f32(D, "shb", dma=True)
        NX = 3
        xt = [ar.f32(D, "xt%d" % k, dma=True) for k in range(NX)]
        t1 = [ar.f32(D, "t1%d" % k) for k in range(2)]
        hb = [ar.bf16(D, "hb%d" % k) for k in range(2)]
        base = (i * 2 + g.idx) * 3 * D
        dma(SP, shb.ap, dram_bcast(modd, base, 128, D), shb, reads=[R_modd], writes=[shb])
        dma(SP, scb.ap, dram_bcast(modd, base + D, 128, D), scb, reads=[R_modd], writes=[scb])
        X, XR = g.X[pos], g.XR[pos]

        def load(tt):
            x = xt[tt % NX]
            dma(SP, x.ap, X[tt * 128:(tt + 1) * 128, :], x, reads=[XR], writes=[x])

        for tt in range(min(NX - 1, nt)):
            load(tt)
        for tt in range(nt):
            if tt + NX - 1 < nt:
                load(tt + NX - 1)
            x = xt[tt % NX]
            t = t1[tt % 2]
            emit(DVE, lambda: nc.vector.tensor_tensor(out=t.ap, in0=x.ap, in1=scb.ap, op=ALU.mult),
                 reads=[x, scb], writes=[t])
            h = hb[tt % 2]
            emit(DVE, lambda: nc.vector.tensor_tensor(out=h.ap, in0=t.ap, in1=shb.ap, op=ALU.add),
                 reads=[t, shb], writes=[h])
            b0 = (tt % 4) * 2
            pbf = psum_t[:, b0 * 512:(b0 + 2) * 512].bitcast(BF16)
            begin(PE, reads=[h, R_cst], writes=[BANK[b0], BANK[b0 + 1]])
            for kc in range(16):
                ins = nc.tensor.transpose(out=pbf[:, kc * 128:(kc + 1) * 128], in_=h.ap[:, kc * 128:(kc + 1) * 128],
                                          identity=ident_bf)
            end(PE, ins, reads=[h, R_cst], writes=[BANK[b0], BANK[b0 + 1]])
            emit(ACT, lambda: nc.scalar.activation(out=hT[:, :, tt * 128:(tt + 1) * 128],
                                                   in_=pbf.rearrange("p (k n) -> p k n", k=16), func=AF.Copy),
                 reads=[BANK[b0], BANK[b0 + 1]], writes=[R_bigA])
        kb.barrier()

    def phase_P_conv(i, g):
        j = i // 2
        T, S, L = g.T, g.S, g.L
        ntq = T // 512
        ar = Arena(kb, arena_t, ARENA_WORDS)
        cu = ar.f32(S * (L + 2), "cu")
        bt = ar.f32(T, "bt")
        zt = ar.f32(T, "zt")
        cvt = ar.f32(T, "cv")
        ut = [ar.f32(512, "ut%d" % k) for k in range(2)]
        yb = [ar.bf16(T, "yb%d" % k, dma=True) for k in range(2)]
        cu3 = cu.ap.rearrange("p (s l) -> p s l", s=S)
        emit(DVE, lambda: nc.vector.memset(cu.ap, 0.0), writes=[cu])

        def seg(ap2d_or_3d_T, tq, is_cu):
            if L >= 512:
                s = (tq * 512) // L
                o = (tq * 512) % L
                if is_cu:
                    return cu3[:, s, 1 + o:1 + o + 512]
                return ap2d_or_3d_T[:, tq * 512:(tq + 1) * 512]
            else:
                ns = 512 // L
                if is_cu:
                    return cu3[:, tq * ns:(tq + 1) * ns, 1:L + 1]
                return ap2d_or_3d_T[:, tq * 512:(tq + 1) * 512].rearrange("p (s l) -> p s l", s=ns)

        def pview(bk):
            if L >= 512:
                return bank_ap(bk)
            return bank_ap(bk).rearrange("p (s l) -> p s l", s=512 // L)

        for fc in range(16):
            k = chunk_ctr[0]
            chunk_ctr[0] += 1
            s = k % 2
            prefetch(k + 1)
            wb = wbuf_t[s]
            for tq in range(ntq):
                bset = (tq % 2) * 4
                for m in (1, 2, 0, 3):
                    bk = bset + m
                    begin(PE, reads=[R_wbuf[s], R_bigA], writes=[BANK[bk]])
                    for kc in range(16):
                        ins = nc.tensor.matmul(bank_ap(bk), lhsT=wb[:, m, kc, :], rhs=hT[:, kc, tq * 512:(tq + 1) * 512],
                                               start=(kc == 0), stop=(kc == 15))
                    end(PE, ins, reads=[R_wbuf[s], R_bigA], writes=[BANK[bk]])
                if fc == 15 and tq == ntq - 1:
                    load_wo(i)
                u = ut[tq % 2]
                emit(ACT, lambda: nc.scalar.activation(out=u.ap, in_=bank_ap(bset + 2), func=AF.Copy),
                     reads=[BANK[bset + 2]], writes=[u])
                uv = u.ap if L >= 512 else u.ap.rearrange("p (s l) -> p s l", s=512 // L)
                emit(DVE, lambda: nc.vector.tensor_tensor(out=seg(None, tq, True), in0=pview(bset + 1), in1=uv,
                                                          op=ALU.mult),
                     reads=[BANK[bset + 1], u], writes=[cu])
                emit(ACT, lambda: nc.scalar.activation(out=bt.ap[:, tq * 512:(tq + 1) * 512], in_=bank_ap(bset + 0),
                                                       func=AF.Copy),
                     reads=[BANK[bset + 0]], writes=[bt])
                emit(ACT, lambda: nc.scalar.activation(out=zt.ap[:, tq * 512:(tq + 1) * 512], in_=bank_ap(bset + 3),
                                                       func=AF.Silu),
                     reads=[BANK[bset + 3]], writes=[zt])
            cw = lambda kk: convw_sb[:, (j * 16 + fc) * 3 + kk:(j * 16 + fc) * 3 + kk + 1]
            cv3 = cvt.ap.rearrange("p (s l) -> p s l", s=S)
            emit(DVE, lambda: nc.vector.tensor_scalar(out=cv3, in0=cu3[:, :, 1:L + 1], scalar1=cw(1), scalar2=None,
                                                      op0=ALU.mult),
                 reads=[cu, R_small], writes=[cvt])
            emit(DVE, lambda: nc.vector.scalar_tensor_tensor(out=cv3, in0=cu3[:, :, 0:L], scalar=cw(0), in1=cv3,
                                                             op0=ALU.mult, op1=ALU.add),
                 reads=[cu, cvt, R_small], writes=[cvt])
            emit(DVE, lambda: nc.vector.scalar_tensor_tensor(out=cv3, in0=cu3[:, :, 2:L + 2], scalar=cw(2), in1=cv3,
                                                             op0=ALU.mult, op1=ALU.add),
                 reads=[cu, cvt, R_small], writes=[cvt])
            emit(DVE, lambda: nc.vector.tensor_tensor(out=cvt.ap, in0=cvt.ap, in1=bt.ap, op=ALU.mult),
                 reads=[cvt, bt], writes=[cvt])
            y = yb[fc % 2]
            emit(DVE, lambda: nc.vector.tensor_tensor(out=y.ap, in0=cvt.ap, in1=zt.ap, op=ALU.mult),
                 reads=[cvt, zt], writes=[y])
            dma(SP, g.Y[fc, :, :], y.ap, y, reads=[y], writes=[g.YR])
        kb.barrier()

    def phase_P_attn(i, g):
        j = i // 2
        T, S, L = g.T, g.S, g.L
        ntq = T // 512
        nkeys = L + (PAST if g.cache else 0)
        nkt = nkeys // 128
        QC = min(512, L)
        nqc = L // QC
        ar = Arena(kb, arena_t, ARENA_WORDS)
        if g.cache:
            cosb = ar.f32(LS, "cos", dma=True)
            sinb = ar.f32(LS, "sin", dma=True)
            dma(SP, cosb.ap, cos_in[:, :], cosb, writes=[cosb])
            dma(SP, sinb.ap, sin_in[:, :], sinb, writes=[sinb])
            ckt = ar.bf16(4 * 128, "ckt", dma=True)
            xb = [ar.bf16(512, "xb%d" % k_) for k_ in range(2)]
            ra = ar.f32(512, "ra")
            rb = ar.f32(512, "rb")
        QT = ar.bf16(T, "QT")
        KT = ar.bf16(S * nkeys, "KT")
        Vb = ar.bf16(S * nkt * 128, "Vb", dma=True)
        Vb3 = Vb.ap.rearrange("p (t e) -> p t e", e=128)
        zs = ar.f32(T, "zs")
        PT = [ar.bf16(1024, "pt%d" % a) for a in range(2)]
        acc = [ar.f32(1024, "acc%d" % a) for a in range(2)]
        PTR = [[Res("ptr%d%d" % (a_, c_)) for c_ in range(2)] for a_ in range(2)]
        accH = {id(a_): [Res("accr0"), Res("accr1")] for a_ in acc}
        NU = 1 if g.cache else 2
        fo01 = [ar.f32(1024, "fo01%d" % u_) for u_ in range(NU)]
        fr = [ar.f32(1024, "fr%d" % u_) for u_ in range(NU)]
        fo = [ar.f32(512, "fo%d" % u_) for u_ in range(NU)]
        fsq = [ar.f32(512, "fsq%d" % u_) for u_ in range(NU)]
        frs = [ar.f32(512, "frs%d" % u_) for u_ in range(NU)]
        ystage = [ar.bf16(T, "yst%d" % k, dma=True) for k in range(2)]
        if not g.cache:
            kst = [ar.f32(T, "kst%d" % k, dma=True) for k in range(2)]
            vst = [ar.f32(T, "vst%d" % k, dma=True) for k in range(2)]
        nlam = smalls[:, C_NLAM + j:C_NLAM + j + 1]
        subw = smalls[:, C_SUBW + j:C_SUBW + j + 1]
        epsc = smalls[:, C_EPS:C_EPS + 1]
        OB0, OB1, SB0, SB1 = 4, 5, 6, 7

        for hd in range(16):
            k = chunk_ctr[0]
            chunk_ctr[0] += 1
            s = k % 2
            prefetch(k + 1)
            wb = wbuf_t[s]
            RW = R_wbuf[s]
            if g.cache:
                dma(POOL, ckt.ap.rearrange("p (t n) -> p t n", t=4),
                    ck_in[j, :, hd * 128:(hd + 1) * 128].rearrange("(t p) n -> p t n", p=128), ckt, writes=[ckt])
                dma(POOL, Vb3[:, 16:20, :],
                    cv_in[j, :, hd * 128:(hd + 1) * 128].rearrange("(t p) n -> p t n", p=128), Vb, writes=[Vb])
            pbanks = [0, 1, 2, 3, 6, 7]
            ropeq = []

            def rope_tail(item):
                bk, dv, tq, xbb = item
                bk2 = pbanks[rot[0] % 6]
                rot[0] += 1
                emit(PE, lambda: nc.tensor.matmul(bank_ap(bk2), lhsT=perm_bf, rhs=xbb.ap, start=True, stop=True),
                     reads=[xbb, R_cst], writes=[BANK[bk2]])
                emit(DVE, lambda: nc.vector.tensor_tensor(out=ra.ap, in0=bank_ap(bk),
                                                          in1=cosb.ap[:, tq * 512:(tq + 1) * 512], op=ALU.mult),
                     reads=[BANK[bk], cosb, xbb], writes=[ra])
                emit(DVE, lambda: nc.vector.tensor_tensor(out=rb.ap, in0=bank_ap(bk2),
                                                          in1=sinb.ap[:, tq * 512:(tq + 1) * 512], op=ALU.mult),
                     reads=[BANK[bk2], sinb], writes=[rb])
                emit(DVE, lambda: nc.vector.tensor_tensor(out=dv[0], in0=ra.ap, in1=rb.ap, op=ALU.add),
                     reads=[ra, rb], writes=[dv[1]])

            ngrp = 0
            for m in ((0, 1) if 'q' not in os.environ.get('KSKIP', '') else ()):
                dest = QT if m == 0 else KT
                for tq in range(ntq):
                    bk = pbanks[rot[0] % 6]
                    rot[0] += 1
                    begin(PE, reads=[RW, R_bigA], writes=[BANK[bk]])
                    for kc in range(16):
                        ins = nc.tensor.matmul(bank_ap(bk), lhsT=wb[:, m, kc, :], rhs=hT[:, kc, tq * 512:(tq + 1) * 512],
                                               start=(kc == 0), stop=(kc == 15))
                    end(PE, ins, reads=[RW, R_bigA], writes=[BANK[bk]])
                    dv = dest.ap[:, tq * 512:(tq + 1) * 512]
                    if not g.cache:
                        emit(ACT, lambda: nc.scalar.activation(out=dv, in_=bank_ap(bk), func=AF.Copy),
                             reads=[BANK[bk]], writes=[dest])
                    else:
                        xbb = xb[ngrp % 2]
                        ngrp += 1
                        emit(ACT, lambda: nc.scalar.activation(out=xbb.ap, in_=bank_ap(bk), func=AF.Copy),
                             reads=[BANK[bk]], writes=[xbb])
                        if ropeq:
                            rope_tail(ropeq.pop(0))
                        ropeq.append((bk, (dv, dest), tq, xbb))
            while ropeq:
                rope_tail(ropeq.pop(0))
            for tq in range(ntq if 'z' not in os.environ.get('KSKIP', '') else 0):
                bk = next_bank()
                begin(PE, reads=[RW, R_bigA], writes=[BANK[bk]])
                for kc in range(16):
                    ins = nc.tensor.matmul(bank_ap(bk), lhsT=wb[:, 3, kc, :], rhs=hT[:, kc, tq * 512:(tq + 1) * 512],
                                           start=(kc == 0), stop=(kc == 15))
                end(PE, ins, reads=[RW, R_bigA], writes=[BANK[bk]])
                emit(ACT, lambda: nc.scalar.activation(out=zs.ap[:, tq * 512:(tq + 1) * 512], in_=bank_ap(bk),
                                                       func=AF.Silu),
                     reads=[BANK[bk]], writes=[zs])
            mats = [2] if g.cache else [2, 1]
            if 'v' in os.environ.get('KSKIP', ''):
                mats = []
            for m in mats:
                for tt in range(T // 128):
                    bk = next_bank()
                    begin(PE, reads=[RW, R_bigA], writes=[BANK[bk]])
                    for kc in range(16):
                        ins = nc.tensor.matmul(bank_ap(bk, 128), lhsT=hT[:, kc, tt * 128:(tt + 1) * 128], rhs=wb[:, m, kc, :],
                                               start=(kc == 0), stop=(kc == 15))
                    end(PE, ins, reads=[RW, R_bigA], writes=[BANK[bk]])
                    tps = L // 128
                    vi = (tt // tps) * nkt + (tt % tps)
                    if not g.cache:
                        st = (vst if m == 2 else kst)[hd % 2]
                        emit(DVE, lambda: nc.vector.tensor_copy(out=st.ap[:, tt * 128:(tt + 1) * 128], in_=bank_ap(bk, 128)),
                             reads=[BANK[bk]], writes=[st])
                        if m == 2:
                            emit(ACT, lambda: nc.scalar.activation(out=Vb3[:, vi, :], in_=st.ap[:, tt * 128:(tt + 1) * 128], func=AF.Copy),
                                 reads=[st], writes=[Vb])
                    else:
                        emit(ACT, lambda: nc.scalar.activation(out=Vb3[:, vi, :], in_=bank_ap(bk, 128), func=AF.Copy),
                             reads=[BANK[bk]], writes=[Vb])
                if not g.cache and 'k' not in os.environ.get('KSKIP', ''):
                    st = (vst if m == 2 else kst)[hd % 2]
                    o_t = nv_out if m == 2 else nk_out
                    st4 = st.ap.rearrange("p (s th e) -> p s th e", s=S, e=128)
                    for s_i in range(S):
                        dst = o_t[s_i, j, :, hd * 128:(hd + 1) * 128].rearrange("(th p) e -> p th e", p=128)
                        dma(SP, dst, st4[:, s_i, :, :], st, reads=[st], writes=[])
            if hd == 15:
                load_wo(i)
            if g.cache:
                bk = next_bank()
                pbf = bank_ap(bk).bitcast(BF16)
                ck3 = ckt.ap.rearrange("p (t n) -> p t n", t=4)
                begin(PE, reads=[ckt, R_cst], writes=[BANK[bk]])
                for t in range(4):
                    ins = nc.tensor.transpose(out=pbf[:, t * 128:(t + 1) * 128], in_=ck3[:, t, :], identity=ident_bf)
                end(PE, ins, reads=[ckt, R_cst], writes=[BANK[bk]])
                emit(ACT, lambda: nc.scalar.activation(out=KT.ap[:, LS:LS + PAST], in_=pbf[:, 0:512], func=AF.Copy),
                     reads=[BANK[bk]], writes=[KT])
            yst = ystage[hd % 2]
            npair = 0
            nchunk = 0
            pending = []
            _skip = os.environ.get('KSKIP', '')
            defer = nkt >= 16 and os.environ.get('KDEFER', '0') == '1'

            def v3w(ap1024, c0, w):
                return ap1024.rearrange("p (c n) -> p c n", c=2)[:, :, c0:c0 + w]

            def make_finish(q0, W, accR, u, sbanks):
                f01, frR, foR, fsqR, frsR = fo01[u], fr[u], fo[u], fsq[u], frs[u]
                sa, sb_ = sbanks

                def st1():
                    for c_, SBk in enumerate((sa, sb_)):
                        emit(PE, lambda: nc.tensor.matmul(bank_ap(SBk, W), lhsT=ones32[:, :], rhs=accR.ap[:, c_ * 512:c_ * 512 + W],
                                                          start=True, stop=True),
                             reads=[accR] + accH[id(accR)], writes=[BANK[SBk]])
                    PE.e.wait_ge(PE.sem, PE.t)

                def st2():
                    sv = psum_t[:, sa * 512:(sa + 2) * 512].rearrange("p (c n) -> p c n", c=2)[:, :, 0:W]
                    emit(ACT, lambda: nc.scalar.activation(out=v3w(frR.ap, 0, W), in_=sv, func=AF.Ln),
                         reads=[BANK[sa], BANK[sb_]], writes=[frR])
                    emit(ACT, lambda: nc.scalar.activation(out=v3w(frR.ap, 0, W), in_=v3w(frR.ap, 0, W), func=AF.Exp, scale=-1.0),
                         reads=[frR], writes=[frR])

                def st3():
                    emit(DVE, lambda: nc.vector.tensor_tensor(out=v3w(f01.ap, 0, W), in0=v3w(f01.ap, 0, W), in1=v3w(frR.ap, 0, W), op=ALU.mult),
                         reads=[f01, frR], writes=[f01])
                    emit(DVE, lambda: nc.vector.scalar_tensor_tensor(out=foR.ap[:, 0:W], in0=f01.ap[:, 512:512 + W], scalar=nlam,
                                                                     in1=f01.ap[:, 0:W], op0=ALU.mult, op1=ALU.add),
                         reads=[f01, R_small], writes=[foR])

                def st4():
                    emit(ACT, lambda: nc.scalar.activation(out=fsqR.ap[:, 0:W], in_=foR.ap[:, 0:W], func=AF.Square),
                         reads=[foR], writes=[fsqR])

                def st5():
                    emit(PE, lambda: nc.tensor.matmul(bank_ap(sa, W), lhsT=ones32[:, :], rhs=fsqR.ap[:, 0:W], start=True, stop=True),
                         reads=[fsqR, R_cst], writes=[BANK[sa]])
                    PE.e.wait_ge(PE.sem, PE.t)

                def st6():
                    emit(ACT, lambda: nc.scalar.activation(out=frsR.ap[:, 0:W], in_=bank_ap(sa, W), func=AF.Ln,
                                                           bias=epsc, scale=1.0 / 128.0),
                         reads=[BANK[sa], R_small], writes=[frsR])
                    emit(ACT, lambda: nc.scalar.activation(out=frsR.ap[:, 0:W], in_=frsR.ap[:, 0:W], func=AF.Exp, scale=-0.5),
                         reads=[frsR], writes=[frsR])

                def st7():
                    emit(DVE, lambda: nc.vector.scalar_tensor_tensor(out=foR.ap[:, 0:W], in0=foR.ap[:, 0:W], scalar=subw,
                                                                     in1=frsR.ap[:, 0:W], op0=ALU.mult, op1=ALU.mult),
                         reads=[foR, frsR, R_small], writes=[foR])
                    emit(DVE, lambda: nc.vector.tensor_tensor(out=yst.ap[:, q0:q0 + W], in0=foR.ap[:, 0:W],
                                                              in1=zs.ap[:, q0:q0 + W], op=ALU.mult),
                         reads=[foR, zs], writes=[yst])
                return [st1, st2, st3, st4, st5, st6, st7]

            for sq in range(S if 'a' not in _skip else 0):
                for qc in range(nqc):
                    q0 = sq * L + qc * QC
                    qv = QT.ap[:, q0:q0 + QC]
                    if g.cache:
                        u, c0 = 0, 0
                        accb = acc[nchunk % 2]
                    else:
                        u, c0 = sq // 2, (sq % 2) * QC
                        accb = acc[u]
                    nchunk += 1
                    accR = accH[id(accb)]

                    def s_mm(kt, pair):
                        ka = KT.ap[:, sq * nkeys + kt * 128:sq * nkeys + (kt + 1) * 128]
                        ba, bb = 2 * pair, 2 * pair + 1
                        begin(PE, reads=[KT, QT], writes=[BANK[ba], BANK[bb]])
                        nc.tensor.matmul(bank_ap(ba, QC), lhsT=ka[0:64, :], rhs=qv[0:64, :], start=True, stop=True)
                        ins = nc.tensor.matmul(bank_ap(bb, QC), lhsT=ka[64:128, :], rhs=qv[64:128, :], start=True, stop=True)
                        end(PE, ins, reads=[KT, QT], writes=[BANK[ba], BANK[bb]])

                    s_mm(0, npair % 2)
                    for kt in range(nkt):
                        pair = npair % 2
                        npair += 1
                        if kt + 1 < nkt:
                            s_mm(kt + 1, npair % 2)
                        ba, bb = 2 * pair, 2 * pair + 1
                        pt = PT[pair]
                        sv = psum_t[:, ba * 512:(ba + 2) * 512].rearrange("p (c n) -> p c n", c=2)[:, :, 0:QC]
                        vt = Vb3[:, sq * nkt + kt, :]
                        first, last = (kt == 0), (kt == nkt - 1)
                        for c_, (bS, bO) in enumerate(((ba, OB0), (bb, OB1))):
                            pR = PTR[pair][c_]
                            pv_ = pt.ap[:, c_ * 512:c_ * 512 + QC]
                            av_ = accb.ap[:, c_ * 512 + c0:c_ * 512 + c0 + QC]
                            emit(ACT, lambda: nc.scalar.activation(out=pv_, in_=bank_ap(bS, QC), func=AF.Exp, scale=0.125),
                                 reads=[BANK[bS]], writes=[pR])
                            emit(PE, lambda: nc.tensor.matmul(bank_ap(bO, QC), lhsT=vt, rhs=pv_, start=first, stop=last),
                                 reads=[pR, Vb], writes=[BANK[bO]])
                            if first:
                                emit(DVE, lambda: nc.vector.tensor_copy(out=av_, in_=pv_), reads=[pR], writes=[accR[c_]])
                            else:
                                emit(DVE, lambda: nc.vector.tensor_tensor(out=av_, in0=av_, in1=pv_, op=ALU.add),
                                     reads=[pR, accR[c_]], writes=[accR[c_]])
                        if defer and pending and kt % 2 == 1:
                            pending.pop(0)()
                    while pending:
                        pending.pop(0)()
                    ov = psum_t[:, OB0 * 512:(OB0 + 2) * 512].rearrange("p (c n) -> p c n", c=2)[:, :, 0:QC]
                    emit(ACT, lambda: nc.scalar.activation(out=v3w(fo01[u].ap, c0, QC), in_=ov, func=AF.Copy),
                         reads=[BANK[OB0], BANK[OB1]], writes=[fo01[u]])
                    if g.cache:
                        pending = make_finish(q0, QC, accb, 0, (SB0, SB1))
                        if not defer:
                            while pending:
                                pending.pop(0)()
            while pending:
                pending.pop(0)()
            if not g.cache and 'a' not in _skip:
                fa = make_finish(0, 512, acc[0], 0, (SB0, SB1))
                fb = make_finish(512, 512, acc[1], 1, (2, 3))
                for sa_, sb2 in zip(fa, fb):
                    sa_()
                    sb2()
            dma(SP, g.Y[hd, :, :], yst.ap, yst, reads=[yst], writes=[g.YR])
        kb.barrier()

    def phase_O(i, g, pos):
        T = g.T
        nt = T // 128
        ar = Arena(kb, arena_t, ARENA_WORDS)
        gb = ar.f32(D, "gb", dma=True)
        gamb = ar.f32(D, "gamb", dma=True)
        betb = ar.f32(D, "betb", dma=True)
        xt = [ar.f32(D, "xt%d" % k, dma=True) for k in range(2)]
        yt = [ar.bf16(16 * 256, "yt%d" % k, dma=True) for k in range(2)]
        tv = [ar.f32(D, "tv%d" % k) for k in range(2)]
        xo = [ar.f32(D, "xo%d" % k, dma=True) for k in range(2)]
        stt = [ar.f32(32, "stt%d" % k) for k in range(2)]
        base = (i * 2 + g.idx) * 3 * D
        dma(SP, gb.ap, dram_bcast(modd, base + 2 * D, 128, D), gb, reads=[R_modd], writes=[gb])
        dma(SP, gamb.ap, dram_bcast(ln_g, i * D, 128, D), gamb, writes=[gamb])
        dma(SP, betb.ap, dram_bcast(ln_b, i * D, 128, D), betb, writes=[betb])
        X, XR = g.X[pos], g.XR[pos]
        XO, XOR = g.X[pos + 1], g.XR[pos + 1]
        epsc = smalls[:, C_EPS:C_EPS + 1]

        def load_y(pr):
            if pr * 2 >= nt:
                return
            y = yt[pr % 2]
            y3 = y.ap.rearrange("p (k t) -> p k t", k=16)
            dma(SP, y3, g.Y[:, :, pr * 256:pr * 256 + 256].rearrange("k p t -> p k t"), y, reads=[g.YR], writes=[y])

        def load(tt):
            x = xt[tt % 2]
            dma(SP, x.ap, X[tt * 128:(tt + 1) * 128, :], x, reads=[XR], writes=[x])

        def stage1(tt):
            y = yt[(tt // 2) % 2]
            y3 = y.ap.rearrange("p (k t) -> p k t", k=16)
            x = xt[tt % 2]
            t_ = tv[tt % 2]
            st = stt[tt % 2]
            bset = (tt % 2) * 4
            for dt in range(4):
                bk = bset + dt
                begin(PE, reads=[y, R_bigA], writes=[BANK[bk]])
                for kc in range(16):
                    ins = nc.tensor.matmul(bank_ap(bk), lhsT=y3[:, kc, (tt % 2) * 128:(tt % 2 + 1) * 128],
                                           rhs=wo[:, kc, dt * 512:(dt + 1) * 512], start=(kc == 0), stop=(kc == 15))
                end(PE, ins, reads=[y, R_bigA], writes=[BANK[bk]])
                emit(DVE, lambda: nc.vector.tensor_tensor(out=t_.ap[:, dt * 512:(dt + 1) * 512], in0=bank_ap(bk),
                                                          in1=gb.ap[:, dt * 512:(dt + 1) * 512], op=ALU.mult),
                     reads=[BANK[bk], gb], writes=[t_])
            emit(DVE, lambda: nc.vector.scalar_tensor_tensor(out=t_.ap, in0=x.ap, scalar=ALPHA, in1=t_.ap,
                                                             op0=ALU.mult, op1=ALU.add),
                 reads=[x, t_], writes=[t_])
            st6 = st.ap[:, 0:24].rearrange("p (c s) -> p c s", c=4)
            begin(DVE, reads=[t_], writes=[st])
            for c4 in range(4):
                ins = nc.vector.bn_stats(out=st6[:, c4, :], in_=t_.ap[:, c4 * 512:(c4 + 1) * 512])
            end(DVE, ins, reads=[t_], writes=[st])
            emit(DVE, lambda: nc.vector.bn_aggr(out=st.ap[:, 24:26], in_=st.ap[:, 0:24]), reads=[st], writes=[st])
            emit(ACT, lambda: nc.scalar.activation(out=st.ap[:, 26:27], in_=st.ap[:, 25:26], func=AF.Sqrt, bias=epsc, scale=1.0),
                 reads=[st, R_small], writes=[st])

        def stage2(tt):
            t_ = tv[tt % 2]
            st = stt[tt % 2]
            o = xo[tt % 2]
            rstd = st.ap[:, 27:28]
            nmr = st.ap[:, 28:29]
            emit(DVE, lambda: nc.vector.reciprocal(out=rstd, in_=st.ap[:, 26:27]), reads=[st], writes=[st])
            emit(DVE, lambda: nc.vector.scalar_tensor_tensor(out=nmr, in0=st.ap[:, 24:25], scalar=-1.0, in1=rstd,
                                                             op0=ALU.mult, op1=ALU.mult),
                 reads=[st], writes=[st])
            emit(ACT, lambda: nc.scalar.activation(out=o.ap, in_=t_.ap, func=AF.Identity, bias=nmr, scale=rstd),
                 reads=[t_, st], writes=[o])
            emit(POOL, lambda: nc.gpsimd.tensor_tensor(out=o.ap, in0=o.ap, in1=gamb.ap, op=ALU.mult),
                 reads=[o, gamb], writes=[o])
            emit(POOL, lambda: nc.gpsimd.tensor_tensor(out=o.ap, in0=o.ap, in1=betb.ap, op=ALU.add),
                 reads=[o, betb], writes=[o])
            dma(SP, XO[tt * 128:(tt + 1) * 128, :], o.ap, o, reads=[o], writes=[XOR])

        load_y(0)
        load(0)
        for tt in range(nt):
            if tt % 2 == 0:
                load_y(tt // 2 + 1)
            if tt + 1 < nt:
                load(tt + 1)
            stage1(tt)
            if tt >= 1:
                stage2(tt - 1)
        stage2(nt - 1)
        kb.barrier()

    prefetch(0)
    for pos, i in enumerate(layer_ids):
        for g in groups:
            phase_H(i, g, pos)
            if i % 2 == 0:
                phase_P_conv(i, g)
            else:
                phase_P_attn(i, g)
            phase_O(i, g, pos)
    kb.barrier()
    return nc


_NC_CACHE = {}


def _rope_tables():
    p = np.arange(128)
    axis = (p % 64) // 32
    half = (p % 32) // 16
    jj = p % 16
    inv = (np.float32(10000.0) ** (-(np.arange(16, dtype=np.float32)) / np.float32(16))).astype(np.float32)
    t = np.arange(LS)
    row = (t // 64).astype(np.float32)
    col = (t % 64).astype(np.float32)
    pos = np.where(axis[:, None] == 0, row[None, :], col[None, :]).astype(np.float32)
    ang = (pos * inv[jj][:, None]).astype(np.float32)
    cosT = np.cos(ang).astype(np.float32)
    sgn = np.where(half == 0, -1.0, 1.0).astype(np.float32)
    sinT = (np.sin(ang).astype(np.float32) * sgn[:, None]).astype(np.float32)
    partner = np.where(half == 0, p + 16, p - 16)
    perm = np.zeros((128, 128), np.float32)
    perm[partner, p] = 1.0
    return cosT, sinT, perm


def make_in_maps(inputs, cores):
    f = lambda a: np.ascontiguousarray(np.asarray(a, dtype=np.float32))
    x_prompt, x_sample = f(inputs["x_prompt"]), f(inputs["x_sample"])
    cache_k, cache_v = f(inputs["cache_k"]), f(inputs["cache_v"])
    c, c_ctx = f(inputs["c"]), f(inputs["c_ctx"])
    cosT, sinT, perm = _rope_tables()
    cmat = np.concatenate([np.eye(128, dtype=np.float32), perm, np.ones((128, 128), np.float32)], axis=1)
    conv_w = f(inputs["conv_w"])
    conv_wT = np.ascontiguousarray(conv_w.reshape(2, 3, 16, 128).transpose(3, 0, 2, 1).reshape(128, 96))
    subwT = np.ascontiguousarray(f(inputs["attn_subln_w"]).T)
    shared = {
        "ada_w": f(inputs["ada_w"]), "ada_b": f(inputs["ada_b"]), "ln_g": f(inputs["ln_g"]), "ln_b": f(inputs["ln_b"]),
        "conv_w_in": f(inputs["conv_w_in"]), "conv_wT": conv_wT, "conv_w_out": f(inputs["conv_w_out"]),
        "attn_w_in": f(inputs["attn_w_in"]), "attn_lam": f(inputs["attn_lambda"]).reshape(2, 256),
        "subwT": subwT, "attn_w_out": f(inputs["attn_w_out"]), "cosT": cosT, "sinT": sinT, "cmat": cmat,
    }
    maps = []
    for b in cores:
        sv = np.stack([c_ctx, c[b]], axis=0)
        svT = np.ascontiguousarray(sv.reshape(2, 16, 128).transpose(2, 1, 0).reshape(128, 32))
        m = dict(shared)
        m["xs"] = x_sample[b]
        m["xp"] = np.ascontiguousarray(x_prompt[4 * b:4 * b + 4].reshape(SP_PER_CORE * LP, D))
        m["ck"] = np.ascontiguousarray(cache_k[b].reshape(2, PAST, D))
        m["cv"] = np.ascontiguousarray(cache_v[b].reshape(2, PAST, D))
        m["svT"] = svT
        maps.append(m)
    return maps


def kernel(**inputs):
    if "nc" not in _NC_CACHE:
        _NC_CACHE["nc"] = build_program()
    nc = _NC_CACHE["nc"]
    cores = list(range(NCORES))
    in_maps = make_in_maps(inputs, cores)
    res = run_bass_kernel_spmd(nc, in_maps, core_ids=cores)
    r = res.results
    y_prompt = np.concatenate([r[b]["yp"].reshape(SP_PER_CORE, LP, D) for b in cores], axis=0).astype(np.float32)
    y_sample = np.stack([r[b]["ys"] for b in cores], axis=0).astype(np.float32)
    nk = np.concatenate([r[b]["nk"].reshape(SP_PER_CORE, 2, LP, 16, 2, 64) for b in cores], axis=0).astype(np.float32)
    nv = np.concatenate([r[b]["nv"].reshape(SP_PER_CORE, 2, LP, 16, 128) for b in cores], axis=0).astype(np.float32)
    return (y_prompt, y_sample, nk, nv)
```
